# Optimizing a Trainium2 kernel written in Bass

```python
import math
import jax, jax.numpy as jnp
from jax import lax
import numpy as np

D_MODEL = 2048
BATCH = 2
SEQ = 4096
DEPTH = 2

D_SSM = D_MODEL // 4
D_ATTN = D_MODEL // 2
D_CONV = D_MODEL // 4
D_MIX = D_SSM + D_ATTN + D_CONV
SSM_GROUP = 16
SSM_GROUPS = D_SSM // SSM_GROUP
SSM_STATE = 64
DT_MIN = 0.001
DT_MAX = 0.1
ATTN_HEAD_DIM = 64
ATTN_HEADS = D_ATTN // (2 * ATTN_HEAD_DIM)
ATTN_V_DIM = 2 * ATTN_HEAD_DIM
Q_BLOCK = 128
ROPE_THETA = 10000.0
CONV_WIDTH = 3
D_FF = 5504
N_BRANCH = 3
NORM_EPS = 1e-6
D_IN = D_SSM + 3 * D_ATTN + 3 * D_CONV
SPLITS = tuple(int(s) for s in np.cumsum([D_SSM, D_ATTN, D_ATTN, D_ATTN, D_CONV, D_CONV]))

kernel_name = "hybrid_s5_diffattn_shortconv_macaron"


def rms_norm(x, g):
    xf = x.astype(jnp.float32)
    y = xf * lax.rsqrt(jnp.mean(xf * xf, axis=-1, keepdims=True) + NORM_EPS)
    return (y * g.astype(jnp.float32)).astype(x.dtype)


def swiglu(h, w13, w2):
    a, b = jnp.split(h @ w13, 2, axis=-1)
    return (jax.nn.silu(a) * b) @ w2


def rope_tables(seq_len, dtype):
    pos = jnp.arange(seq_len, dtype=jnp.float32)
    inv = ROPE_THETA ** (-jnp.arange(0, ATTN_HEAD_DIM, 2, dtype=jnp.float32) / ATTN_HEAD_DIM)
    ang = pos[:, None] * inv[None, :]
    return jnp.cos(ang).astype(dtype), jnp.sin(ang).astype(dtype)


def apply_rope(x, cos, sin):
    half = ATTN_HEAD_DIM // 2
    x1, x2 = x[..., :half], x[..., half:]
    c = cos[:, None, None, :]
    s = sin[:, None, None, :]
    return jnp.concatenate([x1 * c - x2 * s, x2 * c + x1 * s], axis=-1)


def _complex_linear_combine(left, right):
    a1r, a1i, b1r, b1i = left
    a2r, a2i, b2r, b2i = right
    ar = a1r * a2r - a1i * a2i
    ai = a1r * a2i + a1i * a2r
    br = a2r * b1r - a2i * b1i + b2r
    bi = a2r * b1i + a2i * b1r + b2i
    return ar, ai, br, bi


def s5_mixer(u, lam_re, lam_im, log_dt, b_re, b_im, c_re, c_im, d_skip, w_glu, b_glu):
    f32 = jnp.float32
    bsz, seq_len, _ = u.shape
    uf = u.astype(f32)
    ug = uf.reshape(bsz, seq_len, SSM_GROUPS, SSM_GROUP)
    bu_re = jnp.einsum('blgh,gnh->blgn', ug, b_re.astype(f32))
    bu_im = jnp.einsum('blgh,gnh->blgn', ug, b_im.astype(f32))
    state_re = jnp.zeros_like(bu_re)
    state_im = jnp.zeros_like(bu_im)
    for direction, rev in ((0, False), (1, True)):
        lr = lam_re[direction].astype(f32)
        li = lam_im[direction].astype(f32)
        dt = jnp.exp(log_dt[direction].astype(f32))[:, None]
        mag = jnp.exp(dt * lr)
        ar = mag * jnp.cos(dt * li)
        ai = mag * jnp.sin(dt * li)
        denom = lr * lr + li * li
        nr = ar - 1.0
        coef_re = (nr * lr + ai * li) / denom
        coef_im = (ai * lr - nr * li) / denom
        bb_re = coef_re * bu_re - coef_im * bu_im
        bb_im = coef_re * bu_im + coef_im * bu_re
        a_re = jnp.broadcast_to(ar, bb_re.shape)
        a_im = jnp.broadcast_to(ai, bb_im.shape)
        _, _, s_re, s_im = lax.associative_scan(
            _complex_linear_combine, (a_re, a_im, bb_re, bb_im), reverse=rev, axis=1)
        state_re = state_re + s_re
        state_im = state_im + s_im
    y = (jnp.einsum('blgn,ghn->blgh', state_re, c_re.astype(f32))
         - jnp.einsum('blgn,ghn->blgh', state_im, c_im.astype(f32)))
    y = y.reshape(bsz, seq_len, D_SSM) + d_skip.astype(f32) * uf
    y = jax.nn.gelu(y)
    y = y * jax.nn.sigmoid(y @ w_glu.astype(f32) + b_glu.astype(f32))
    return y.astype(u.dtype)


def diff_attention(q, k, v, cos, sin, lam_vec, subln_g, lambda_init):
    bsz, seq_len, _ = q.shape
    q = apply_rope(q.reshape(bsz, seq_len, ATTN_HEADS, 2, ATTN_HEAD_DIM), cos, sin)
    k = apply_rope(k.reshape(bsz, seq_len, ATTN_HEADS, 2, ATTN_HEAD_DIM), cos, sin)
    v = v.reshape(bsz, seq_len, ATTN_HEADS, ATTN_V_DIM)
    lv = lam_vec.astype(jnp.float32)
    lam = jnp.exp(jnp.sum(lv[0] * lv[1])) - jnp.exp(jnp.sum(lv[2] * lv[3])) + lambda_init
    scale = ATTN_HEAD_DIM ** -0.5
    n_blocks = seq_len // Q_BLOCK
    qb = q.reshape(bsz, n_blocks, Q_BLOCK, ATTN_HEADS, 2, ATTN_HEAD_DIM).transpose(1, 0, 2, 3, 4, 5)

    def attend_block(q_blk):
        s = jnp.einsum('bqhcd,bkhcd->bhcqk', q_blk, k).astype(jnp.float32) * scale
        p = jax.nn.softmax(s, axis=-1)
        w = p[:, :, 0] - lam * p[:, :, 1]
        return jnp.einsum('bhqk,bkhd->bqhd', w.astype(v.dtype), v)

    o = lax.map(attend_block, qb)
    o = o.transpose(1, 0, 2, 3, 4).reshape(bsz, seq_len, ATTN_HEADS, ATTN_V_DIM)
    o = rms_norm(o, subln_g) * (1.0 - lambda_init)
    return o.reshape(bsz, seq_len, D_ATTN)


def short_conv(bg, cg, xv, conv_w):
    z = cg * xv
    zc = lax.conv_general_dilated(
        z, conv_w[:, None, :].astype(z.dtype), window_strides=(1,),
        padding=((CONV_WIDTH // 2, CONV_WIDTH // 2),),
        dimension_numbers=('NWC', 'WIO', 'NWC'), feature_group_count=D_CONV)
    return bg * zc


def setup_inputs(seed: int = 0) -> dict:
    key = jax.random.key(seed)
    ks = jax.random.split(key, 26)
    f32 = jnp.float32
    nrm = lambda k, shape, s: jax.random.normal(k, shape, f32) * s
    x = jax.random.normal(ks[0], (BATCH, SEQ, D_MODEL), f32)
    norm_w = 1.0 + nrm(ks[1], (DEPTH, 3, D_MODEL), 0.01)
    ffn_w13 = nrm(ks[2], (DEPTH, 2, D_MODEL, 2 * D_FF), D_MODEL ** -0.5)
    ffn_w2 = nrm(ks[3], (DEPTH, 2, D_FF, D_MODEL), D_FF ** -0.5)
    w_in = nrm(ks[4], (DEPTH, D_MODEL, D_IN), D_MODEL ** -0.5)
    s5_lambda_re = -0.5 + nrm(ks[5], (DEPTH, 2, SSM_GROUPS, SSM_STATE), 0.01)
    s5_lambda_im = (math.pi * jnp.arange(SSM_STATE, dtype=f32)
                    + nrm(ks[6], (DEPTH, 2, SSM_GROUPS, SSM_STATE), 0.01))
    s5_log_dt = jax.random.uniform(ks[7], (DEPTH, 2, SSM_GROUPS), f32,
                                   minval=math.log(DT_MIN), maxval=math.log(DT_MAX))
    s5_b_re = nrm(ks[8], (DEPTH, SSM_GROUPS, SSM_STATE, SSM_GROUP), SSM_GROUP ** -0.5)
    s5_b_im = nrm(ks[9], (DEPTH, SSM_GROUPS, SSM_STATE, SSM_GROUP), SSM_GROUP ** -0.5)
    s5_c_re = nrm(ks[10], (DEPTH, SSM_GROUPS, SSM_GROUP, SSM_STATE), SSM_STATE ** -0.5)
    s5_c_im = nrm(ks[11], (DEPTH, SSM_GROUPS, SSM_GROUP, SSM_STATE), SSM_STATE ** -0.5)
    s5_d = nrm(ks[12], (DEPTH, D_SSM), 1.0)
    s5_w_glu = nrm(ks[13], (DEPTH, D_SSM, D_SSM), D_SSM ** -0.5)
    s5_b_glu = nrm(ks[14], (DEPTH, D_SSM), 0.01)
    diff_lambda = nrm(ks[15], (DEPTH, 4, ATTN_HEAD_DIM), 0.1)
    diff_subln = 1.0 + nrm(ks[16], (DEPTH, ATTN_V_DIM), 0.01)
    conv_w = nrm(ks[17], (DEPTH, CONV_WIDTH, D_CONV), CONV_WIDTH ** -0.5)
    w_branch = jnp.concatenate([
        nrm(ks[18], (DEPTH, D_SSM, D_MODEL), D_SSM ** -0.5),
        nrm(ks[19], (DEPTH, D_ATTN, D_MODEL), D_ATTN ** -0.5),
        nrm(ks[20], (DEPTH, D_CONV, D_MODEL), D_CONV ** -0.5)], axis=1)
    w_gate = nrm(ks[21], (DEPTH, D_MODEL, N_BRANCH * D_MODEL), D_MODEL ** -0.5)
    b_gate = nrm(ks[22], (DEPTH, N_BRANCH * D_MODEL), 0.01)
    w_out = nrm(ks[23], (DEPTH, D_MODEL, D_MODEL), D_MODEL ** -0.5)
    final_norm = 1.0 + nrm(ks[24], (D_MODEL,), 0.01)
    return {"x": x, "norm_w": norm_w, "ffn_w13": ffn_w13, "ffn_w2": ffn_w2, "w_in": w_in,
            "s5_lambda_re": s5_lambda_re, "s5_lambda_im": s5_lambda_im, "s5_log_dt": s5_log_dt,
            "s5_b_re": s5_b_re, "s5_b_im": s5_b_im, "s5_c_re": s5_c_re, "s5_c_im": s5_c_im,
            "s5_d": s5_d, "s5_w_glu": s5_w_glu, "s5_b_glu": s5_b_glu,
            "diff_lambda": diff_lambda, "diff_subln": diff_subln, "conv_w": conv_w,
            "w_branch": w_branch, "w_gate": w_gate, "b_gate": b_gate, "w_out": w_out,
            "final_norm": final_norm}


def reference(x, norm_w, ffn_w13, ffn_w2, w_in, s5_lambda_re, s5_lambda_im, s5_log_dt,
              s5_b_re, s5_b_im, s5_c_re, s5_c_im, s5_d, s5_w_glu, s5_b_glu,
              diff_lambda, diff_subln, conv_w, w_branch, w_gate, b_gate, w_out, final_norm):
    bsz, seq_len, _ = x.shape
    cos, sin = rope_tables(seq_len, x.dtype)
    r_a, r_b = D_SSM, D_SSM + D_ATTN
    for l in range(DEPTH):
        lambda_init = 0.8 - 0.6 * math.exp(-0.3 * l)
        x = x + 0.5 * swiglu(rms_norm(x, norm_w[l, 0]), ffn_w13[l, 0], ffn_w2[l, 0])
        h = rms_norm(x, norm_w[l, 1])
        u_ssm, q, k, v, bg, cg, xv = jnp.split(h @ w_in[l], SPLITS, axis=-1)
        y_a = s5_mixer(u_ssm, s5_lambda_re[l], s5_lambda_im[l], s5_log_dt[l],
                       s5_b_re[l], s5_b_im[l], s5_c_re[l], s5_c_im[l],
                       s5_d[l], s5_w_glu[l], s5_b_glu[l])
        y_b = diff_attention(q, k, v, cos, sin, diff_lambda[l], diff_subln[l], lambda_init)
        y_c = short_conv(bg, cg, xv, conv_w[l])
        p_a = y_a @ w_branch[l, :r_a]
        p_b = y_b @ w_branch[l, r_a:r_b]
        p_c = y_c @ w_branch[l, r_b:]
        g = jax.nn.sigmoid(h @ w_gate[l] + b_gate[l]).reshape(bsz, seq_len, N_BRANCH, D_MODEL)
        merged = g[:, :, 0] * p_a + g[:, :, 1] * p_b + g[:, :, 2] * p_c
        x = x + merged @ w_out[l]
        x = x + 0.5 * swiglu(rms_norm(x, norm_w[l, 2]), ffn_w13[l, 1], ffn_w2[l, 1])
    return rms_norm(x, final_norm)
```

```python
import math
import bisect
from contextlib import ExitStack
import numpy as np
import ml_dtypes
import concourse.bass as bass
import concourse.mybir as mybir
from concourse.bass_utils import run_bass_kernel_spmd

F32 = mybir.dt.float32
BF16 = mybir.dt.bfloat16
AF = mybir.ActivationFunctionType
ALU = mybir.AluOpType
AX = mybir.AxisListType

D_MODEL = 2048
SEQ = 4096
BATCH = 2
DEPTH = 2
D_FF = 5504
D_IN = 5120
NTOK = 1024
NKT = 16
NFT = 43
EPS = 1e-6
PI = math.pi
TWO_PI = 2.0 * math.pi


class Ticket:
    __slots__ = ("eng", "idx", "sem", "value")

    def __init__(self, eng, idx, sem, value):
        self.eng, self.idx, self.sem, self.value = eng, idx, sem, value


class Buf:
    def __init__(self, name, excl=False):
        self.name = name
        self.writer = None
        self.readers = {}
        self.dma_sem = None
        self.dma_cnt = 0
        self.excl = excl


class Eng:
    def __init__(self, name, handle, sem):
        self.name, self.h, self.sem = name, handle, sem
        self.count = 0
        self.n = 0
        self.last = None
        self.last_sig = True
        self.sig_idx = []
        self.sig_val = []
        self.known = {}

    def materialize(self, t):
        if t.value is not None:
            return
        i = bisect.bisect_left(self.sig_idx, t.idx)
        if i < len(self.sig_idx):
            t.value = self.sig_val[i]
            return
        assert not self.last_sig
        self.count += 1
        self.last.then_inc(self.sem, 1)
        self.last_sig = True
        self.sig_idx.append(self.n - 1)
        self.sig_val.append(self.count)
        t.value = self.count


class FW:
    def __init__(self, nc, st):
        self.nc = nc
        self.st = st
        self.engs = {}
        self.dma_bufs = []
        for n, h in (("pe", nc.tensor), ("act", nc.scalar), ("dve", nc.vector), ("pool", nc.gpsimd), ("sp", nc.sync)):
            self.engs[n] = Eng(n, h, self.new_sem("e_" + n))

    def new_sem(self, name):
        return self.st.enter_context(self.nc.semaphore(name))

    def _wait(self, e, t):
        if t.eng is not None:
            if t.eng is e and e.name == "pe":
                return
            t.eng.materialize(t)
        key = id(t.sem)
        if e.known.get(key, 0) >= t.value:
            return
        e.h.wait_ge(t.sem, t.value)
        e.known[key] = t.value

    def _deps(self, e, reads, writes):
        for b in reads:
            if b.writer is not None:
                self._wait(e, b.writer)
        for b in writes:
            if b.writer is not None:
                self._wait(e, b.writer)
            for r in b.readers.values():
                if r.eng is e:
                    continue
                self._wait(e, r)

    def _track(self, t, key, reads, writes):
        for b in reads:
            b.readers[key] = t
        for b in writes:
            b.writer = t
            b.readers = {}

    def op(self, ename, fn, reads=(), writes=(), signal=True):
        e = self.engs[ename]
        rd = [b for b in reads if not b.excl]
        wr = list(writes) + [b for b in reads if b.excl]
        self._deps(e, rd, wr)
        ins = fn(e.h)
        idx = e.n
        e.n += 1
        e.last = ins
        if signal:
            e.count += 1
            ins.then_inc(e.sem, 1)
            e.sig_idx.append(idx)
            e.sig_val.append(e.count)
            e.last_sig = True
            t = Ticket(e, idx, e.sem, e.count)
        else:
            e.last_sig = False
            t = Ticket(e, idx, e.sem, None)
        self._track(t, ename, rd, wr)
        return ins

    def dma(self, ename, out, in_, dst, src, **kw):
        e = self.engs[ename]
        self._deps(e, [src], [dst])
        if dst.dma_sem is None:
            dst.dma_sem = self.new_sem("d_" + dst.name)
            self.dma_bufs.append(dst)
        dst.dma_cnt += 16
        ins = e.h.dma_start(out=out, in_=in_, **kw)
        ins.then_inc(dst.dma_sem, 16)
        t = Ticket(None, -1, dst.dma_sem, dst.dma_cnt)
        self._track(t, "dma_" + dst.name, [src], [dst])
        return ins

    def barrier(self):
        ts = []
        for e2 in self.engs.values():
            if e2.n > 0:
                ts.append(Ticket(e2, e2.n - 1, e2.sem, None))
        for b in self.dma_bufs:
            ts.append(Ticket(None, -1, b.dma_sem, b.dma_cnt))
        for e in self.engs.values():
            for t in ts:
                if t.eng is e:
                    continue
                self._wait(e, t)

    def final_wait(self, ename, bufs):
        e = self.engs[ename]
        for b in bufs:
            if b.writer is not None:
                self._wait(e, b.writer)


class Ctx:
    def __init__(self, name):
        self.nc = bass.Bass("TRN2", target_bir_lowering=False)
        self.st = ExitStack()
        self.fw = None
        self.name = name
        self.cnt = 0
        self.out_bufs = []
        self.banks = []
        self.bank_i = 0

    def start(self):
        self.fw = FW(self.nc, self.st)

    def din(self, name, shape, dt=F32):
        return self.nc.dram_tensor(name, list(shape), dt, kind="ExternalInput").ap(), Buf("i_" + name)

    def dout(self, name, shape, dt=F32):
        b = Buf("o_" + name)
        self.out_bufs.append(b)
        return self.nc.dram_tensor(name, list(shape), dt, kind="ExternalOutput").ap(), b

    def sb(self, name, shape, dt=F32):
        return self.st.enter_context(self.nc.sbuf_tensor(name, list(shape), dt)), Buf(name)

    def ps(self, name, shape, dt=F32):
        return self.st.enter_context(self.nc.psum_tensor(name, list(shape), dt)), Buf(name, excl=True)

    def make_banks(self, n):
        for i in range(n):
            self.banks.append(self.ps(f"bank{i}", [128, 512], F32))

    def bank(self):
        t = self.banks[self.bank_i % len(self.banks)]
        self.bank_i += 1
        return t


def mm(fw, out, lhsT, rhs, start, stop, reads, writes):
    return fw.op("pe", lambda p: p.matmul(out, lhsT=lhsT, rhs=rhs, start=start, stop=stop),
                 reads=reads, writes=writes, signal=stop)


class WeightStream:
    def __init__(self, cx, nslots=2, slot_elems=8192):
        self.cx = cx
        self.slots = [cx.sb(f"wslot{i}", [128, slot_elems], BF16) for i in range(nslots)]
        self.i = 0

    def load(self, parts):
        t, B = self.slots[self.i % len(self.slots)]
        self.i += 1
        views = []
        off = 0
        for (dap, dbuf, nk, ncols) in parts:
            v = t[:, off:off + nk * ncols].rearrange("p (k c) -> p k c", c=ncols)
            self.cx.fw.dma("pool", v, dap, B, dbuf)
            views.append(v)
            off += nk * ncols
        assert off <= 8192
        return views, B


MAGIC = 12582912.0
C1_2PI = 6.28125
C2_2PI = TWO_PI - 6.28125


def emit_round(fw, eng, out, Bout, in_, Bin, scale, offset):
    if offset != 0.0:
        fw.op(eng, lambda v: v.tensor_scalar(out, in_, scale, offset, ALU.mult, ALU.add), reads=[Bin], writes=[Bout])
        fw.op(eng, lambda v: v.tensor_scalar_add(out, out, MAGIC), reads=[Bout], writes=[Bout])
    else:
        fw.op(eng, lambda v: v.tensor_scalar(out, in_, scale, MAGIC, ALU.mult, ALU.add), reads=[Bin], writes=[Bout])
    fw.op(eng, lambda v: v.tensor_scalar_add(out, out, -MAGIC), reads=[Bout], writes=[Bout])


def emit_sin(fw, out, Bout, x, Bx, phase, k, Bk):
    if phase != 0.0:
        fw.op("dve", lambda v: v.tensor_scalar_add(out, x, phase), reads=[Bx], writes=[Bout])
        x, Bx = out, Bout
    emit_round(fw, "dve", k, Bk, x, Bx, 1.0 / TWO_PI, 0.0)
    fw.op("dve", lambda v: v.scalar_tensor_tensor(out, k, -C1_2PI, x, ALU.mult, ALU.add), reads=[Bk, Bx], writes=[Bout])
    fw.op("dve", lambda v: v.scalar_tensor_tensor(out, k, -C2_2PI, out, ALU.mult, ALU.add), reads=[Bk, Bout], writes=[Bout])
    fw.op("dve", lambda v: v.tensor_scalar(out, out, -PI, PI, ALU.max, ALU.min), reads=[Bout], writes=[Bout])
    fw.op("act", lambda a: a.activation(out, out, AF.Sin), reads=[Bout], writes=[Bout])


def emit_consts(cx):
    fw = cx.fw
    ones, Bones = cx.sb("ones", [128, 128], F32)
    fw.op("dve", lambda v: v.memset(ones[:], 1.0), writes=[Bones])
    cx.ones, cx.Bones = ones, Bones
    epsc, Bepsc = cx.sb("epsc", [128, 1], F32)
    fw.op("dve", lambda v: v.memset(epsc[:], EPS), writes=[Bepsc])
    cx.epsc, cx.Bepsc = epsc, Bepsc


def emit_rmsnorm(cx, X, BX, g, Bg, H, BH):
    fw = cx.fw
    for half in range(2):
        sl = slice(half * 512, (half + 1) * 512)
        pb, Bpb = cx.bank()
        for kt in range(NKT):
            sq, Bsq = cx.sqbufs[kt % 2]
            fw.op("act", lambda a, sq=sq, kt=kt: a.activation(sq[:], X[:, kt, sl], AF.Square), reads=[BX], writes=[Bsq])
            mm(fw, pb[:], cx.ones[:], sq[:], kt == 0, kt == NKT - 1, [cx.Bones, Bsq], [Bpb])
        rstd, Brstd = cx.rstd
        fw.op("act", lambda a: a.activation(rstd[:], pb[:], AF.Sqrt, bias=cx.epsc[:, 0:1], scale=1.0 / D_MODEL), reads=[Bpb, cx.Bepsc], writes=[Brstd])
        fw.op("dve", lambda v: v.reciprocal(rstd[:], rstd[:]), reads=[Brstd], writes=[Brstd])
        for kt in range(NKT):
            eng = "dve"
            fw.op(eng, lambda v, kt=kt: v.scalar_tensor_tensor(H[:, kt, sl], X[:, kt, sl], g[:, kt:kt + 1], rstd[:], ALU.mult, ALU.mult),
                  reads=[BX, Bg, Brstd], writes=[BH])


def emit_ffn(cx, ws, X, BX, H, BH, w13, Bw13, w2, Bw2, ACTB, BACT):
    fw = cx.fw
    w13v = w13.rearrange("(kt p) c -> p kt c", p=128)
    quarters = [(0, 11), (11, 22), (22, 33), (33, 43)]
    for (f0, f1) in quarters:
        nft = f1 - f0
        ft = f0
        while ft < f1:
            ng = min(2, f1 - ft)
            nc_ = ng * 128
            (va, vb), Bs = ws.load([(w13v[:, :, ft * 128:ft * 128 + nc_], Bw13, NKT, nc_),
                                    (w13v[:, :, D_FF + ft * 128:D_FF + ft * 128 + nc_], Bw13, NKT, nc_)])
            for gi in range(ng):
                for half in range(2):
                    sl = slice(half * 512, (half + 1) * 512)
                    pa, Bpa = cx.bank()
                    pbb, Bpbb = cx.bank()
                    for kt in range(NKT):
                        mm(fw, pa[:], va[:, kt, gi * 128:(gi + 1) * 128], H[:, kt, sl], kt == 0, kt == NKT - 1, [Bs, BH], [Bpa])
                    for kt in range(NKT):
                        mm(fw, pbb[:], vb[:, kt, gi * 128:(gi + 1) * 128], H[:, kt, sl], kt == 0, kt == NKT - 1, [Bs, BH], [Bpbb])
                    sa, Bsa = cx.tmp512[cx.cnt % 2]
                    cx.cnt += 1
                    fw.op("act", lambda a, sa=sa, pa=pa: a.activation(sa[:], pa[:], AF.Silu), reads=[Bpa], writes=[Bsa])
                    fl = ft + gi - f0
                    fw.op("dve", lambda v, sa=sa, pbb=pbb, fl=fl: v.tensor_tensor(ACTB[:, fl, sl], sa[:], pbb[:], ALU.mult),
                          reads=[Bsa, Bpbb], writes=[BACT])
            ft += ng
        w2v = w2[f0 * 128:f1 * 128, :].rearrange("(kt p) c -> p kt c", p=128)
        for cg in range(8):
            (vw,), Bs = ws.load([(w2v[:, :, cg * 256:(cg + 1) * 256], Bw2, nft, 256)])
            for gi in range(2):
                m = cg * 2 + gi
                for half in range(2):
                    sl = slice(half * 512, (half + 1) * 512)
                    po, Bpo = cx.bank()
                    for kt in range(nft):
                        mm(fw, po[:], vw[:, kt, gi * 128:(gi + 1) * 128], ACTB[:, kt, sl], kt == 0, kt == nft - 1, [Bs, BACT], [Bpo])
                    fw.op("dve", lambda v, po=po, m=m: v.scalar_tensor_tensor(X[:, m, sl], po[:], 0.5, X[:, m, sl], ALU.mult, ALU.add),
                          reads=[Bpo, BX], writes=[BX])


def load_X(cx, X, BX, xT, BxT):
    xv = xT.rearrange("(kt p) t -> p kt t", p=128)
    for kt in range(0, NKT, 4):
        cx.fw.dma("sp", X[:, kt:kt + 4, :], xv[:, kt:kt + 4, :], BX, BxT)


def store_X(cx, X, BX, xT, BxT):
    xv = xT.rearrange("(kt p) t -> p kt t", p=128)
    for kt in range(0, NKT, 4):
        cx.fw.dma("sp", xv[:, kt:kt + 4, :], X[:, kt:kt + 4, :], BxT, BX)


def build_A():
    cx = Ctx("A")
    nc = cx.nc
    xT, BxT = cx.din("xT", [D_MODEL, NTOK])
    gn, Bgn_d = cx.din("gn", [128, 2 * NKT])
    pos, Bpos_d = cx.din("pos", [128, NTOK])
    w13, Bw13 = cx.din("w13", [D_MODEL, 2 * D_FF])
    w2, Bw2 = cx.din("w2", [D_FF, D_MODEL])
    win, Bwin = cx.din("win", [D_MODEL, D_IN])
    x1T, Bx1T = cx.dout("x1T", [D_MODEL, NTOK])
    qkT, BqkT = cx.dout("qkT", [2048, NTOK], BF16)
    vT, BvT = cx.dout("vT", [1024, NTOK], BF16)
    restT, BrestT = cx.dout("restT", [2048, NTOK])
    with cx.st:
        cx.start()
        fw = cx.fw
        cx.make_banks(8)
        X, BX = cx.sb("X", [128, NKT, NTOK], F32)
        H, BH = cx.sb("H", [128, NKT, NTOK], BF16)
        ACTB, BACT = cx.sb("ACTB", [128, 11, NTOK], BF16)
        g, Bg = cx.sb("g", [128, 2 * NKT], F32)
        cx.sqbufs = [cx.sb(f"sq{i}", [128, 512], F32) for i in range(2)]
        cx.rstd = cx.sb("rstd", [128, 512], F32)
        cx.tmp512 = [cx.sb(f"tmp{i}", [128, 512], F32) for i in range(2)]
        ws = WeightStream(cx)
        emit_consts(cx)
        fw.dma("sp", g[:], gn[:, :], Bg, Bgn_d)
        load_X(cx, X, BX, xT, BxT)
        pidx, Bpidx = cx.sb("pidx", [128, 1], F32)
        fw.op("pool", lambda gp: gp.iota(pidx[:], pattern=[[0, 1]], base=0, channel_multiplier=1,
                                         allow_small_or_imprecise_dtypes=True), writes=[Bpidx])
        invf, Binvf = cx.sb("invf", [128, 1], F32)
        kk, Bkk = cx.sb("kk", [128, 1], F32)
        emit_round(fw, "dve", kk[:], Bkk, pidx[:], Bpidx, 1.0 / 32.0, -0.484375)
        fw.op("dve", lambda v: v.scalar_tensor_tensor(invf[:], kk[:], -32.0, pidx[:], ALU.mult, ALU.add), reads=[Bkk, Bpidx], writes=[Binvf])
        fw.op("act", lambda a: a.activation(invf[:], invf[:], AF.Exp, scale=-math.log(10000.0) / 32.0), reads=[Binvf], writes=[Binvf])
        mlow, Bmlow = cx.sb("mlow", [128, 1], F32)
        emit_round(fw, "dve", kk[:], Bkk, pidx[:], Bpidx, 1.0 / 64.0, -0.4921875)
        fw.op("dve", lambda v: v.scalar_tensor_tensor(mlow[:], kk[:], -64.0, pidx[:], ALU.mult, ALU.add), reads=[Bkk, Bpidx], writes=[Bmlow])
        fw.op("dve", lambda v: v.tensor_single_scalar(mlow[:], mlow[:], 32.0, ALU.is_lt), reads=[Bmlow], writes=[Bmlow])
        mhigh, Bmhigh = cx.sb("mhigh", [128, 1], F32)
        fw.op("dve", lambda v: v.tensor_scalar(mhigh[:], mlow[:], -1.0, 1.0, ALU.mult, ALU.add), reads=[Bmlow], writes=[Bmhigh])
        sgn, Bsgn = cx.sb("sgn", [128, 1], F32)
        fw.op("dve", lambda v: v.tensor_scalar(sgn[:], mlow[:], -2.0, 1.0, ALU.mult, ALU.add), reads=[Bmlow], writes=[Bsgn])
        Ct, BCt = cx.sb("Ct", [128, NTOK], F32)
        St, BSt = cx.sb("St", [128, NTOK], F32)
        ang, Bang = cx.sb("ang", [128, NTOK], F32)
        kt_, Bkt_ = cx.sb("ktmp", [128, NTOK], F32)
        fw.dma("sp", St[:], pos[:, :], BSt, Bpos_d)
        fw.op("dve", lambda v: v.tensor_scalar(ang[:], St[:], invf[:, 0:1], None, ALU.mult), reads=[BSt, Binvf], writes=[Bang])
        emit_sin(fw, St[:], BSt, ang[:], Bang, 0.0, kt_[:], Bkt_)
        fw.op("dve", lambda v: v.tensor_scalar(St[:], St[:], sgn[:, 0:1], None, ALU.mult), reads=[BSt, Bsgn], writes=[BSt])
        emit_sin(fw, Ct[:], BCt, ang[:], Bang, 0.5 * PI, kt_[:], Bkt_)
        dmat, Bdmat = cx.sb("dmat", [128, 128], F32)
        fw.op("pool", lambda gp: gp.iota(dmat[:], pattern=[[1, 128]], base=0, channel_multiplier=-1,
                                         allow_small_or_imprecise_dtypes=True), writes=[Bdmat])
        Pm, BPm = cx.sb("Pm", [128, 128], F32)
        e2, Be2 = cx.sb("e2", [128, 128], F32)
        fw.op("dve", lambda v: v.tensor_scalar(Pm[:], dmat[:], 32.0, mlow[:, 0:1], ALU.is_equal, ALU.mult), reads=[Bdmat, Bmlow], writes=[BPm])
        fw.op("dve", lambda v: v.tensor_scalar(e2[:], dmat[:], -32.0, mhigh[:, 0:1], ALU.is_equal, ALU.mult), reads=[Bdmat, Bmhigh], writes=[Be2])
        fw.op("dve", lambda v: v.tensor_tensor(Pm[:], Pm[:], e2[:], ALU.add), reads=[BPm, Be2], writes=[BPm])

        emit_rmsnorm(cx, X, BX, g[:, 0:NKT], Bg, H, BH)
        emit_ffn(cx, ws, X, BX, H, BH, w13, Bw13, w2, Bw2, ACTB, BACT)
        store_X(cx, X, BX, x1T, Bx1T)
        emit_rmsnorm(cx, X, BX, g[:, NKT:2 * NKT], Bg, H, BH)
        winv = win.rearrange("(kt p) c -> p kt c", p=128)
        stg32 = [cx.sb(f"stg32_{i}", [128, 512], F32) for i in range(3)]
        stg16 = [cx.sb(f"stg16_{i}", [128, 512], BF16) for i in range(3)]
        t1b = [cx.sb(f"t1b_{i}", [128, 512], F32) for i in range(2)]
        si = 0
        for cg in range(20):
            (vw,), Bs = ws.load([(winv[:, :, cg * 256:(cg + 1) * 256], Bwin, NKT, 256)])
            for gi in range(2):
                ct = cg * 2 + gi
                for half in range(2):
                    sl = slice(half * 512, (half + 1) * 512)
                    pb, Bpb = cx.bank()
                    for kt in range(NKT):
                        mm(fw, pb[:], vw[:, kt, gi * 128:(gi + 1) * 128], H[:, kt, sl], kt == 0, kt == NKT - 1, [Bs, BH], [Bpb])
                    si += 1
                    if ct < 4 or ct >= 28:
                        s32, Bs32 = stg32[si % 3]
                        fw.op("act", lambda a, s32=s32, pb=pb: a.copy(s32[:], pb[:]), reads=[Bpb], writes=[Bs32])
                        row = ct * 128 if ct < 4 else (ct - 28 + 4) * 128
                        fw.dma("sp", restT[row:row + 128, sl], s32[:], BrestT, Bs32)
                    elif ct < 20:
                        s32, Bs32 = stg32[si % 3]
                        fw.op("act", lambda a, s32=s32, pb=pb: a.copy(s32[:], pb[:]), reads=[Bpb], writes=[Bs32])
                        pp, Bpp = cx.bank()
                        mm(fw, pp[:], Pm[:], s32[:], True, True, [BPm, Bs32], [Bpp])
                        ta, Bta = t1b[si % 2]
                        fw.op("pool", lambda gp, ta=ta, s32=s32: gp.tensor_tensor(ta[:], s32[:], Ct[:, sl], ALU.mult), reads=[Bs32, BCt], writes=[Bta])
                        tb, Btb = cx.tmp512[si % 2]
                        fw.op("dve", lambda v, tb=tb, pp=pp: v.tensor_tensor(tb[:], pp[:], St[:, sl], ALU.mult), reads=[Bpp, BSt], writes=[Btb])
                        s16, Bs16 = stg16[si % 3]
                        fw.op("dve", lambda v, s16=s16, ta=ta, tb=tb: v.tensor_tensor(s16[:], ta[:], tb[:], ALU.add), reads=[Bta, Btb], writes=[Bs16])
                        row = (ct - 4) * 128
                        fw.dma("sp", qkT[row:row + 128, sl], s16[:], BqkT, Bs16)
                    else:
                        s16, Bs16 = stg16[si % 3]
                        fw.op("act", lambda a, s16=s16, pb=pb: a.copy(s16[:], pb[:]), reads=[Bpb], writes=[Bs16])
                        row = (ct - 20) * 128
                        fw.dma("sp", vT[row:row + 128, sl], s16[:], BvT, Bs16)
        fw.final_wait("sp", cx.out_bufs)
    return nc


_CACHE = {}


def _get(name, builder):
    if name not in _CACHE:
        _CACHE[name] = builder()
    return _CACHE[name]


def _run(nc, in_maps):
    res = run_bass_kernel_spmd(nc, in_maps, core_ids=list(range(8)))
    return res.results


def _gains(norm_w_l, idxs):
    return np.ascontiguousarray(np.concatenate([norm_w_l[i].reshape(NKT, 128).T for i in idxs], axis=1))


def run_A(xT_shards, inputs, l):
    nc = _get("A", build_A)
    gn = _gains(inputs["norm_w"][l], [0, 1])
    in_maps = []
    for c in range(8):
        p0 = (c % 4) * NTOK
        pos = np.ascontiguousarray(np.broadcast_to(np.arange(p0, p0 + NTOK, dtype=np.float32)[None, :], (128, NTOK)))
        in_maps.append({"xT": xT_shards[c], "gn": gn, "pos": pos,
                        "w13": np.ascontiguousarray(inputs["ffn_w13"][l, 0]), "w2": np.ascontiguousarray(inputs["ffn_w2"][l, 0]),
                        "win": np.ascontiguousarray(inputs["w_in"][l])})
    return _run(nc, in_maps)


TS = 256
NCH = SEQ // TS


def rev_ap(a):
    n = a.ap[-1][1]
    st = a.ap[-1][0]
    return bass.AP(tensor=a.tensor, offset=a.offset + (n - 1) * st, ap=[list(x) for x in a.ap[:-1]] + [[-st, n]])


def build_M():
    cx = Ctx("M")
    qT, BqT = cx.din("qT", [256, SEQ], BF16)
    kT, BkT = cx.din("kT", [256, SEQ], BF16)
    vv, Bvv = cx.din("v", [SEQ, 256], BF16)
    uT, BuT = cx.din("uT", [128, SEQ])
    bgT, BbgT = cx.din("bgT", [128, SEQ])
    cgT, BcgT = cx.din("cgT", [128, SEQ])
    xvT, BxvT = cx.din("xvT", [128, SEQ])
    lamc, Blamc = cx.din("lamc", [128, 24])
    lamr, Blamr = cx.din("lamr", [128, 2 * 3 * 512])
    BTd, BBTd = cx.din("BT", [128, 2 * 512])
    CTd, BCTd = cx.din("CT", [128, 2 * 4 * 128])
    smalls, Bsmalls = cx.din("smalls", [128, 1 + 3 + 2])
    dl, Bdl = cx.din("dl", [128, 256])
    sub, Bsub = cx.din("subln", [128, 128])
    yaT, ByaT = cx.dout("yaT", [128, SEQ])
    yb, Byb = cx.dout("yb", [SEQ, 256])
    ycT, BycT = cx.dout("ycT", [128, SEQ])
    with cx.st:
        cx.start()
        fw = cx.fw
        BP = Buf("params")
        sm, Bsm = cx.sb("sm", [128, 6])
        fw.dma("sp", sm[:], smalls[:, :], Bsm, Bsmalls)
        epsc, Bepsc = cx.sb("epsc", [128, 1])
        fw.op("dve", lambda v: v.memset(epsc[:], EPS), writes=[Bepsc])

        with ExitStack() as tst:
            def tsb(name, shape, dt=F32):
                return tst.enter_context(cx.nc.sbuf_tensor(name, list(shape), dt)), Buf(name)
            cg, Bcg = tsb("cg", [128, SEQ])
            xv, Bxv = tsb("xv", [128, SEQ])
            bg, Bbg = tsb("bg", [128, SEQ])
            fw.dma("sp", cg[:], cgT[:, :], Bcg, BcgT)
            fw.dma("sp", xv[:], xvT[:, :], Bxv, BxvT)
            fw.dma("sp", bg[:], bgT[:, :], Bbg, BbgT)
            fw.op("pool", lambda g: g.tensor_tensor(cg[:], cg[:], xv[:], ALU.mult), reads=[Bcg, Bxv], writes=[Bcg])
            fw.op("dve", lambda v: v.tensor_scalar(xv[:], cg[:], sm[:, 2:3], None, ALU.mult), reads=[Bcg, Bsm], writes=[Bxv])
            fw.op("dve", lambda v: v.scalar_tensor_tensor(xv[:, 1:SEQ], cg[:, 0:SEQ - 1], sm[:, 1:2], xv[:, 1:SEQ], ALU.mult, ALU.add),
                  reads=[Bcg, Bsm, Bxv], writes=[Bxv])
            fw.op("dve", lambda v: v.scalar_tensor_tensor(xv[:, 0:SEQ - 1], cg[:, 1:SEQ], sm[:, 3:4], xv[:, 0:SEQ - 1], ALU.mult, ALU.add),
                  reads=[Bcg, Bsm, Bxv], writes=[Bxv])
            fw.op("pool", lambda g: g.tensor_tensor(bg[:], bg[:], xv[:], ALU.mult), reads=[Bbg, Bxv], writes=[Bbg])
            fw.dma("sp", ycT[:, :], bg[:], BycT, Bbg)
            fw.barrier()

        cx.banks = []
        accs = [cx.ps(f"acc{i}", [128, 512], F32) for i in range(2)]
        sbanks = [cx.ps(f"sbank{i}", [128, 512], F32) for i in range(2)]
        psBU, BpsBU = cx.ps("psBU", [128, 2, 512], F32)
        psY, BpsY = cx.ps("psY", [128, 512], F32)
        qs, Bqs = cx.sb("qs", [128, 2, SEQ], BF16)
        ks, Bks = cx.sb("ks", [128, 2, SEQ], BF16)
        Vaug, BVaug = cx.sb("Vaug", [128, 2, 32, 130], BF16)
        u, Bu = cx.sb("u", [128, SEQ])
        yacc, Byacc = cx.sb("yacc", [128, SEQ])
        fw.dma("sp", qs[:], qT.rearrange("(h p) t -> p h t", p=128), Bqs, BqT)
        fw.dma("sp", ks[:], kT.rearrange("(h p) t -> p h t", p=128), Bks, BkT)
        fw.op("pool", lambda g: g.memset(Vaug[:], 1.0), writes=[BVaug])
        for hh in range(2):
            vsrc = vv[:, hh * 128:(hh + 1) * 128].rearrange("(kt p) d -> p kt d", p=128)
            for k8 in range(4):
                fw.dma("sp", Vaug[:, hh, k8 * 8:(k8 + 1) * 8, 0:128], vsrc[:, k8 * 8:(k8 + 1) * 8, :], BVaug, Bvv)
        fw.dma("sp", u[:], uT[:, :], Bu, BuT)

        dls, Bdls = cx.sb("dls", [128, 256])
        fw.dma("sp", dls[:], dl[:, :], Bdls, Bdl)
        gsc, Bgsc = cx.sb("gsc", [128, 128])
        fw.dma("sp", gsc[:], sub[:, :], Bgsc, Bsub)
        fw.op("dve", lambda v: v.tensor_scalar(gsc[:], gsc[:], sm[:, 5:6], None, ALU.mult), reads=[Bgsc, Bsm], writes=[Bgsc])
        pr, Bpr = cx.sb("pr", [128, 128])
        s2, Bs2 = cx.sb("s2", [128, 2])
        fw.op("dve", lambda v: v.tensor_tensor(pr[:, 0:64], dls[:, 0:64], dls[:, 64:128], ALU.mult), reads=[Bdls], writes=[Bpr])
        fw.op("dve", lambda v: v.tensor_tensor(pr[:, 64:128], dls[:, 128:192], dls[:, 192:256], ALU.mult), reads=[Bdls, Bpr], writes=[Bpr])
        fw.op("dve", lambda v: v.reduce_sum(s2[:, 0:1], pr[:, 0:64], axis=AX.X), reads=[Bpr], writes=[Bs2])
        fw.op("dve", lambda v: v.reduce_sum(s2[:, 1:2], pr[:, 64:128], axis=AX.X), reads=[Bpr, Bs2], writes=[Bs2])
        fw.op("act", lambda a: a.activation(s2[:], s2[:], AF.Exp), reads=[Bs2], writes=[Bs2])
        neglam, Bneglam = cx.sb("neglam", [128, 1])
        fw.op("dve", lambda v: v.tensor_tensor(neglam[:], s2[:, 1:2], s2[:, 0:1], ALU.subtract), reads=[Bs2], writes=[Bneglam])
        fw.op("dve", lambda v: v.tensor_tensor(neglam[:], neglam[:], sm[:, 4:5], ALU.subtract), reads=[Bneglam, Bsm], writes=[Bneglam])

        lc, _ = cx.sb("lc", [128, 24])
        big, _ = cx.sb("big", [128, 8192])
        lr_ = big[:, 0:3072]
        BT, _ = cx.sb("BTs", [128, 1024])
        CT, _ = cx.sb("CTs", [128, 1024])
        fw.dma("sp", lc[:], lamc[:, :], BP, Blamc)
        fw.dma("sp", lr_, lamr[:, :], BP, Blamr)
        fw.dma("sp", BT[:], BTd[:, :], BP, BBTd)
        fw.dma("sp", CT[:], CTd[:, :], BP, BCTd)
        CTv = CT[:].rearrange("p (c s h) -> p c s h", c=2, s=4)

        def P(eng, f):
            fw.op(eng, f, reads=[BP], writes=[BP])
        P("dve", lambda v: v.tensor_scalar(CT[:, 512:1024], CT[:, 512:1024], -1.0, None, ALU.mult))
        Jf, _ = cx.sb("Jf", [128, TS])
        Jb, _ = cx.sb("Jb", [128, TS])
        P("pool", lambda g: g.iota(Jf[:], pattern=[[1, TS]], base=0, channel_multiplier=0, allow_small_or_imprecise_dtypes=True))
        P("pool", lambda g: g.iota(Jb[:], pattern=[[-1, TS]], base=TS - 1, channel_multiplier=0, allow_small_or_imprecise_dtypes=True))
        mf, _ = cx.sb("mf", [128, TS])
        mb, _ = cx.sb("mb", [128, TS])
        P("dve", lambda v: v.memset(mf[:], 1.0))
        P("dve", lambda v: v.memset(mf[:, 0:1], 0.0))
        P("dve", lambda v: v.memset(mb[:], 1.0))
        P("dve", lambda v: v.memset(mb[:, TS - 1:TS], 0.0))
        cosT = [cx.sb(f"cosT{d}", [128, 4, TS])[0] for d in range(2)]
        sinT = [cx.sb(f"sinT{d}", [128, 4, TS])[0] for d in range(2)]
        rmask = [cx.sb(f"rmask{d}", [128, 4, TS])[0] for d in range(2)]
        G = [cx.sb(f"G{d}", [128, 2, 4])[0] for d in range(2)]
        BbT = [cx.sb(f"BbT{d}", [128, 2, 512])[0] for d in range(2)]
        dtc, _ = cx.sb("dtc", [128, 4])
        thc, _ = cx.sb("thc", [128, 4])
        rhoc, _ = cx.sb("rhoc", [128, 4])
        angc, _ = cx.sb("angc", [128, 4])
        kc, _ = cx.sb("kc", [128, 4])
        ang, _ = cx.sb("angt", [128, TS])
        kt_, _ = cx.sb("ktt", [128, TS])
        class _V:
            def __init__(self, ap):
                self.ap = ap

            def __getitem__(self, k):
                return self.ap
        r1, r2, r3, r4, r5, r6, kr = [_V(big[:, 3072 + i * 512:3072 + (i + 1) * 512]) for i in range(7)]
        for d in range(2):
            lrc = lc[:, d * 12 + 0:d * 12 + 4]
            lic = lc[:, d * 12 + 4:d * 12 + 8]
            ldc = lc[:, d * 12 + 8:d * 12 + 12]
            P("act", lambda a: a.activation(dtc[:], ldc, AF.Exp))
            P("dve", lambda v: v.tensor_tensor(thc[:], dtc[:], lic, ALU.mult))
            P("dve", lambda v: v.tensor_tensor(rhoc[:], dtc[:], lrc, ALU.mult))
            P("act", lambda a: a.activation(rhoc[:], rhoc[:], AF.Exp))
            J = Jf if d == 0 else Jb
            msk = mf if d == 0 else mb
            for s in range(4):
                P("dve", lambda v, s=s: v.tensor_scalar(ang[:], J[:], thc[:, s:s + 1], None, ALU.mult))
                emit_sin(fw, sinT[d][:, s, :], BP, ang[:], BP, 0.0, kt_[:], BP)
                emit_sin(fw, cosT[d][:, s, :], BP, ang[:], BP, 0.5 * PI, kt_[:], BP)
                P("dve", lambda v, s=s: v.tensor_scalar(rmask[d][:, s, :], msk[:], rhoc[:, s:s + 1], None, ALU.mult))
            P("dve", lambda v: v.tensor_scalar(angc[:], thc[:], float(TS), None, ALU.mult))
            emit_sin(fw, G[d][:, 1, :], BP, angc[:], BP, 0.0, kc[:], BP)
            emit_sin(fw, G[d][:, 0, :], BP, angc[:], BP, 0.5 * PI, kc[:], BP)
            P("dve", lambda v: v.tensor_tensor(G[d][:, 0, :], G[d][:, 0, :], rhoc[:], ALU.mult))
            P("dve", lambda v: v.tensor_tensor(G[d][:, 1, :], G[d][:, 1, :], rhoc[:], ALU.mult))
            lrr = lr_[:, d * 1536 + 0:d * 1536 + 512]
            lir = lr_[:, d * 1536 + 512:d * 1536 + 1024]
            ldr = lr_[:, d * 1536 + 1024:d * 1536 + 1536]
            P("act", lambda a: a.activation(r1[:], ldr, AF.Exp))
            P("dve", lambda v: v.tensor_tensor(r2[:], r1[:], lrr, ALU.mult))
            P("act", lambda a: a.activation(r2[:], r2[:], AF.Exp))
            P("dve", lambda v: v.tensor_tensor(r3[:], r1[:], lir, ALU.mult))
            emit_sin(fw, r4[:], BP, r3[:], BP, 0.0, kr[:], BP)
            emit_sin(fw, r5[:], BP, r3[:], BP, 0.5 * PI, kr[:], BP)
            P("dve", lambda v: v.tensor_tensor(r4[:], r4[:], r2[:], ALU.mult))
            P("dve", lambda v: v.tensor_tensor(r5[:], r5[:], r2[:], ALU.mult))
            P("dve", lambda v: v.tensor_scalar_add(r5[:], r5[:], -1.0))
            P("dve", lambda v: v.tensor_tensor(r1[:], lrr, lrr, ALU.mult))
            P("dve", lambda v: v.tensor_tensor(r2[:], lir, lir, ALU.mult))
            P("dve", lambda v: v.tensor_tensor(r1[:], r1[:], r2[:], ALU.add))
            P("dve", lambda v: v.reciprocal(r1[:], r1[:]))
            P("dve", lambda v: v.tensor_tensor(r2[:], r5[:], lrr, ALU.mult))
            P("dve", lambda v: v.tensor_tensor(r3[:], r4[:], lir, ALU.mult))
            P("dve", lambda v: v.tensor_tensor(r2[:], r2[:], r3[:], ALU.add))
            P("dve", lambda v: v.tensor_tensor(r2[:], r2[:], r1[:], ALU.mult))
            P("dve", lambda v: v.tensor_tensor(r3[:], r4[:], lrr, ALU.mult))
            P("dve", lambda v: v.tensor_tensor(r6[:], r5[:], lir, ALU.mult))
            P("dve", lambda v: v.tensor_tensor(r3[:], r3[:], r6[:], ALU.subtract))
            P("dve", lambda v: v.tensor_tensor(r3[:], r3[:], r1[:], ALU.mult))
            P("dve", lambda v: v.tensor_tensor(r4[:], BT[:, 0:512], r2[:], ALU.mult))
            P("dve", lambda v: v.tensor_tensor(r5[:], BT[:, 512:1024], r3[:], ALU.mult))
            P("dve", lambda v: v.tensor_tensor(BbT[d][:, 0, :], r4[:], r5[:], ALU.subtract))
            P("dve", lambda v: v.tensor_tensor(r4[:], BT[:, 0:512], r3[:], ALU.mult))
            P("dve", lambda v: v.tensor_tensor(r5[:], BT[:, 512:1024], r2[:], ALU.mult))
            P("dve", lambda v: v.tensor_tensor(BbT[d][:, 1, :], r4[:], r5[:], ALU.add))

        fw.barrier()
        def _wv(i):
            return big[:, i * 2048:(i + 1) * 2048].rearrange("p (c s t) -> p c s t", c=2, s=4)
        W, BW = _wv(0), Buf("W")
        R, BR = _wv(1), Buf("R")
        Ss = [(_wv(2), Buf("S0")), (_wv(3), Buf("S1"))]
        tq = [cx.sb(f"tq{i}", [128, 2, TS]) for i in range(4)]
        uq = [cx.sb(f"uq{i}", [128, 4, TS]) for i in range(4)]
        cp, Bcp = cx.sb("cp", [128, 2, 4])
        cq, Bcq = cx.sb("cq", [128, 4, 4])
        pts = [cx.sb(f"pt{i}", [128, 256], BF16) for i in range(3)]
        o0 = [cx.sb(f"o0_{i}", [128, 128]) for i in range(2)]
        o1 = [cx.sb(f"o1_{i}", [128, 128]) for i in range(2)]
        ybs = [cx.sb(f"ybs{i}", [128, 128]) for i in range(2)]
        junk, Bjunk = cx.sb("junk", [128, 128])
        rs_ = [cx.sb(f"rs{i}", [128, 4]) for i in range(2)]

        def ssm_unit(ui):
            d = ui // NCH
            ci = ui % NCH
            c = ci if d == 0 else NCH - 1 - ci
            t0 = c * TS
            first = 0 if d == 0 else TS - 1
            last = TS - 1 if d == 0 else 0
            S, BS = Ss[ui % 2]
            if ci > 0:
                fw.op("pool", lambda g: g.tensor_tensor(cq[:, 0, :], G[d][:, 0, :], R[:, 0, :, last], ALU.mult), reads=[BP, BR], writes=[Bcq])
                fw.op("pool", lambda g: g.tensor_tensor(cq[:, 1, :], G[d][:, 1, :], R[:, 1, :, last], ALU.mult), reads=[BP, BR, Bcq], writes=[Bcq])
                fw.op("pool", lambda g: g.tensor_tensor(cq[:, 2, :], G[d][:, 0, :], R[:, 1, :, last], ALU.mult), reads=[BP, BR, Bcq], writes=[Bcq])
                fw.op("pool", lambda g: g.tensor_tensor(cq[:, 3, :], G[d][:, 1, :], R[:, 0, :, last], ALU.mult), reads=[BP, BR, Bcq], writes=[Bcq])
                fw.op("pool", lambda g: g.tensor_tensor(cp[:, 0, :], cq[:, 0, :], cq[:, 1, :], ALU.subtract), reads=[Bcq], writes=[Bcp])
                fw.op("pool", lambda g: g.tensor_tensor(cp[:, 1, :], cq[:, 2, :], cq[:, 3, :], ALU.add), reads=[Bcq, Bcp], writes=[Bcp])
            for hf in range(2):
                for si in range(2):
                    s = hf * 2 + si
                    mm(fw, psBU[:, si, 0:TS], BbT[d][:, 0, s * 128:(s + 1) * 128], u[:, t0:t0 + TS], True, True, [BP, Bu], [BpsBU])
                    mm(fw, psBU[:, si, TS:2 * TS], BbT[d][:, 1, s * 128:(s + 1) * 128], u[:, t0:t0 + TS], True, True, [BP, Bu], [BpsBU])
                bre = psBU[:, :, 0:TS]
                bim = psBU[:, :, TS:2 * TS]
                cs = cosT[d][:, hf * 2:hf * 2 + 2, :]
                sn = sinT[d][:, hf * 2:hf * 2 + 2, :]
                (t1, B1), (t2, B2), (t3, B3), (t4, B4) = tq
                fw.op("dve", lambda v: v.tensor_tensor(t1[:], bre, cs, ALU.mult), reads=[BpsBU, BP], writes=[B1])
                fw.op("dve", lambda v: v.tensor_tensor(t2[:], bim, sn, ALU.mult), reads=[BpsBU, BP], writes=[B2])
                fw.op("dve", lambda v: v.tensor_tensor(t3[:], bim, cs, ALU.mult), reads=[BpsBU, BP], writes=[B3])
                fw.op("dve", lambda v: v.tensor_tensor(t4[:], bre, sn, ALU.mult), reads=[BpsBU, BP], writes=[B4])
                fw.op("pool", lambda g: g.tensor_tensor(W[:, 0, hf * 2:hf * 2 + 2, :], t1[:], t2[:], ALU.add), reads=[B1, B2], writes=[BW])
                fw.op("pool", lambda g: g.tensor_tensor(W[:, 1, hf * 2:hf * 2 + 2, :], t3[:], t4[:], ALU.subtract), reads=[B3, B4], writes=[BW])
            if ci > 0:
                fw.op("pool", lambda g: g.tensor_tensor(W[:, 0, :, first], W[:, 0, :, first], cp[:, 0, :], ALU.add), reads=[BW, Bcp], writes=[BW])
                fw.op("pool", lambda g: g.tensor_tensor(W[:, 1, :, first], W[:, 1, :, first], cp[:, 1, :], ALU.add), reads=[BW, Bcp], writes=[BW])
            for c2 in range(2):
                o_ = R[:, c2, :, :].rearrange("p s t -> p (s t)")
                m_ = rmask[d][:].rearrange("p s t -> p (s t)")
                w_ = W[:, c2, :, :].rearrange("p s t -> p (s t)")
                if d == 1:
                    o_, m_, w_ = rev_ap(o_), rev_ap(m_), rev_ap(w_)
                fw.op("dve", lambda v, o_=o_, m_=m_, w_=w_: v.tensor_tensor_scan(o_, m_, w_, 0.0, ALU.mult, ALU.add), reads=[BP, BW], writes=[BR])
            (u1, Bu1), (u2, Bu2), (u3, Bu3), (u4, Bu4) = uq
            fw.op("pool", lambda g: g.tensor_tensor(u1[:], R[:, 0, :, :], cosT[d][:], ALU.mult), reads=[BR, BP], writes=[Bu1])
            fw.op("pool", lambda g: g.tensor_tensor(u2[:], R[:, 1, :, :], sinT[d][:], ALU.mult), reads=[BR, BP], writes=[Bu2])
            fw.op("pool", lambda g: g.tensor_tensor(S[:, 0, :, :], u1[:], u2[:], ALU.subtract), reads=[Bu1, Bu2], writes=[BS])
            fw.op("dve", lambda v: v.tensor_tensor(u3[:], R[:, 1, :, :], cosT[d][:], ALU.mult), reads=[BR, BP], writes=[Bu3])
            fw.op("dve", lambda v: v.tensor_tensor(u4[:], R[:, 0, :, :], sinT[d][:], ALU.mult), reads=[BR, BP], writes=[Bu4])
            fw.op("dve", lambda v: v.tensor_tensor(S[:, 1, :, :], u3[:], u4[:], ALU.add), reads=[Bu3, Bu4], writes=[BS])
            return (S, BS, d, t0)

        def ssm_back(st):
            S, BS, d, t0 = st
            n = 0
            for s in range(4):
                for c2 in range(2):
                    mm(fw, psY[:, 0:TS], CTv[:, c2, s, :], S[:, c2, s, :], n == 0, n == 7, [BP, BS], [BpsY])
                    n += 1
            if d == 0:
                fw.op("act", lambda a: a.copy(yacc[:, t0:t0 + TS], psY[:, 0:TS]), reads=[BpsY], writes=[Byacc])
            else:
                fw.op("dve", lambda v: v.tensor_tensor(yacc[:, t0:t0 + TS], psY[:, 0:TS], yacc[:, t0:t0 + TS], ALU.add), reads=[BpsY, Byacc], writes=[Byacc])

        def attn_unit(ai):
            hh = ai // 32
            qb = (ai // 2) % 16
            c = ai % 2
            q0 = qb * 256
            for kt in range(32):
                psS, BpsS = sbanks[kt % 2]
                mm(fw, psS[:, 0:256], ks[c * 64:(c + 1) * 64, hh, kt * 128:(kt + 1) * 128], qs[c * 64:(c + 1) * 64, hh, q0:q0 + 256],
                   True, True, [Bks, Bqs], [BpsS])
                pt, Bpt = pts[kt % 3]
                fw.op("act", lambda a, pt=pt, psS=psS: a.activation(pt[:], psS[:, 0:256], AF.Exp, scale=0.125), reads=[BpsS], writes=[Bpt])
                for sb_ in range(2):
                    acc, Bacc = accs[sb_]
                    mm(fw, acc[:, 0:129], pt[:, sb_ * 128:(sb_ + 1) * 128], Vaug[:, hh, kt, 0:129], kt == 0, kt == 31, [Bpt, BVaug], [Bacc])
            for sb_ in range(2):
                acc, Bacc = accs[sb_]
                rs, Brs = rs_[sb_]
                fw.op("dve", lambda v, rs=rs, acc=acc: v.reciprocal(rs[:, 0:1], acc[:, 128:129]), reads=[Bacc], writes=[Brs])
                oo, Boo = o0[sb_]
                if c == 0:
                    fw.op("dve", lambda v, oo=oo, acc=acc, rs=rs: v.tensor_scalar(oo[:], acc[:, 0:128], rs[:, 0:1], None, ALU.mult), reads=[Bacc, Brs], writes=[Boo])
                else:
                    o_, Bo_ = o1[sb_]
                    fw.op("dve", lambda v, rs=rs: v.tensor_tensor(rs[:, 1:2], rs[:, 0:1], neglam[:], ALU.mult), reads=[Brs, Bneglam], writes=[Brs])
                    fw.op("dve", lambda v, o_=o_, acc=acc, rs=rs, oo=oo: v.scalar_tensor_tensor(o_[:], acc[:, 0:128], rs[:, 1:2], oo[:], ALU.mult, ALU.add),
                          reads=[Bacc, Brs, Boo], writes=[Bo_])
                    fw.op("act", lambda a, o_=o_: a.activation(junk[:], o_[:], AF.Square), reads=[Bo_], writes=[Bjunk])
                    fw.op("dve", lambda v, rs=rs: v.reduce_sum(rs[:, 2:3], junk[:], axis=AX.X), reads=[Bjunk, Brs], writes=[Brs])
                    fw.op("act", lambda a, rs=rs: a.activation(rs[:, 3:4], rs[:, 2:3], AF.Sqrt, bias=epsc[:, 0:1], scale=1.0 / 128.0), reads=[Brs, Bepsc], writes=[Brs])
                    fw.op("dve", lambda v, rs=rs: v.reciprocal(rs[:, 3:4], rs[:, 3:4]), reads=[Brs], writes=[Brs])
                    yy, Byy = ybs[sb_]
                    fw.op("dve", lambda v, yy=yy, o_=o_, rs=rs: v.scalar_tensor_tensor(yy[:], o_[:], rs[:, 3:4], gsc[:], ALU.mult, ALU.mult),
                          reads=[Bo_, Brs, Bgsc], writes=[Byy])
                    r0 = q0 + sb_ * 128
                    fw.dma("sp", yb[r0:r0 + 128, hh * 128:(hh + 1) * 128], yy[:], Byb, Byy)

        pending = None
        for ai in range(64):
            attn_unit(ai)
            if ai % 2 == 1:
                stt_ = ssm_unit(ai // 2)
                if pending is not None:
                    ssm_back(pending)
                pending = stt_
        ssm_back(pending)

        ga, Bga = cx.sb("ga", [128, 1024])
        gb, Bgb = cx.sb("gb", [128, 1024])
        for blk in range(4):
            sl = slice(blk * 1024, (blk + 1) * 1024)
            fw.op("dve", lambda v: v.scalar_tensor_tensor(yacc[:, sl], u[:, sl], sm[:, 0:1], yacc[:, sl], ALU.mult, ALU.add), reads=[Bu, Bsm, Byacc], writes=[Byacc])
            fw.op("act", lambda a: a.activation(ga[:], yacc[:, sl], AF.Square), reads=[Byacc], writes=[Bga])
            fw.op("dve", lambda v: v.tensor_scalar(ga[:], ga[:], 0.044715, 1.0, ALU.mult, ALU.add), reads=[Bga], writes=[Bga])
            fw.op("pool", lambda g: g.tensor_tensor(gb[:], ga[:], yacc[:, sl], ALU.mult), reads=[Bga, Byacc], writes=[Bgb])
            fw.op("act", lambda a: a.activation(gb[:], gb[:], AF.Sigmoid, scale=2.0 * math.sqrt(2.0 / math.pi)), reads=[Bgb], writes=[Bgb])
            fw.op("pool", lambda g: g.tensor_tensor(gb[:], gb[:], yacc[:, sl], ALU.mult), reads=[Bgb, Byacc], writes=[Bgb])
            fw.dma("sp", yaT[:, sl], gb[:], ByaT, Bgb)
        fw.final_wait("sp", cx.out_bufs)
    return cx.nc


def _bf16(a):
    return np.ascontiguousarray(a).astype(ml_dtypes.bfloat16)


def make_M_inputs(inputs, l, q_b, k_b, v_b, u_b, bg_b, cg_b, xv_b):
    lam_init = 0.8 - 0.6 * math.exp(-0.3 * l)
    in_maps = []
    lre = inputs["s5_lambda_re"][l]; lim = inputs["s5_lambda_im"][l]; ldt = inputs["s5_log_dt"][l]
    for c in range(8):
        b, j = c // 4, c % 4
        gs = slice(8 * j, 8 * j + 8)
        lamc = np.zeros((128, 24), np.float32)
        lamr = np.zeros((2 * 3 * 512,), np.float32)
        for d in range(2):
            kinds = [lre[d, gs].reshape(512), lim[d, gs].reshape(512), np.repeat(ldt[d, gs], 64)]
            for ki, arr in enumerate(kinds):
                lamc[:, d * 12 + ki * 4:d * 12 + ki * 4 + 4] = arr.reshape(4, 128).T
                lamr[d * 1536 + ki * 512:d * 1536 + (ki + 1) * 512] = arr
        lamr = np.ascontiguousarray(np.broadcast_to(lamr[None, :], (128, 3072)))
        BT = np.zeros((128, 2, 512), np.float32)
        CT = np.zeros((128, 2, 4, 128), np.float32)
        for gl in range(8):
            g = 8 * j + gl
            for ri, (bsrc, csrc) in enumerate(((inputs["s5_b_re"], inputs["s5_c_re"]), (inputs["s5_b_im"], inputs["s5_c_im"]))):
                BT[gl * 16:(gl + 1) * 16, ri, gl * 64:(gl + 1) * 64] = bsrc[l, g].T
                tile, p0 = (gl * 64) // 128, (gl * 64) % 128
                CT[p0:p0 + 64, ri, tile, gl * 16:(gl + 1) * 16] = csrc[l, g].T
        smalls = np.zeros((128, 6), np.float32)
        smalls[:, 0] = inputs["s5_d"][l, 128 * j:128 * j + 128]
        smalls[:, 1:4] = inputs["conv_w"][l][:, 128 * j:128 * j + 128].T
        smalls[:, 4] = np.float32(lam_init)
        smalls[:, 5] = np.float32(1.0 - lam_init)
        dl = np.ascontiguousarray(np.broadcast_to(inputs["diff_lambda"][l].reshape(1, 256), (128, 256)))
        sub = np.ascontiguousarray(np.broadcast_to(inputs["diff_subln"][l].reshape(1, 128), (128, 128)))
        hs = slice(256 * j, 256 * j + 256)
        cs = slice(128 * j, 128 * j + 128)
        in_maps.append({
            "qT": np.ascontiguousarray(q_b[b][:, hs].T), "kT": np.ascontiguousarray(k_b[b][:, hs].T),
            "v": np.ascontiguousarray(v_b[b][:, hs]),
            "uT": np.ascontiguousarray(u_b[b][:, cs].T), "bgT": np.ascontiguousarray(bg_b[b][:, cs].T),
            "cgT": np.ascontiguousarray(cg_b[b][:, cs].T), "xvT": np.ascontiguousarray(xv_b[b][:, cs].T),
            "lamc": lamc, "lamr": lamr, "BT": BT.reshape(128, 1024), "CT": CT.reshape(128, 1024),
            "smalls": smalls, "dl": dl, "subln": sub})
    return in_maps


def run_M(inputs, l, q_b, k_b, v_b, u_b, bg_b, cg_b, xv_b):
    nc = _get("M", build_M)
    res = _run(nc, make_M_inputs(inputs, l, q_b, k_b, v_b, u_b, bg_b, cg_b, xv_b))
    ya = [np.concatenate([res[b * 4 + j]["yaT"].T for j in range(4)], axis=1) for b in range(2)]
    ybb = [np.concatenate([res[b * 4 + j]["yb"] for j in range(4)], axis=1) for b in range(2)]
    yc = [np.concatenate([res[b * 4 + j]["ycT"].T for j in range(4)], axis=1) for b in range(2)]
    return ya, ybb, yc


def build_C(last):
    cx = Ctx("C1" if last else "C0")
    x1T, Bx1T = cx.din("x1T", [D_MODEL, NTOK])
    yaT, ByaT = cx.din("yaT", [512, NTOK])
    ybT, BybT = cx.din("ybT", [1024, NTOK])
    ycT, BycT = cx.din("ycT", [512, NTOK])
    gn, Bgn_d = cx.din("gn", [128, 3 * NKT])
    wglu, Bwglu = cx.din("wglu", [512, 512])
    bsm, Bbsm_d = cx.din("bsm", [128, 4 + 48])
    wbr, Bwbr = cx.din("wbr", [D_MODEL, D_MODEL])
    wg, Bwg = cx.din("wg", [D_MODEL, 3 * D_MODEL])
    wout, Bwout = cx.din("wout", [D_MODEL, D_MODEL])
    w13, Bw13 = cx.din("w13", [D_MODEL, 2 * D_FF])
    w2, Bw2 = cx.din("w2", [D_FF, D_MODEL])
    xoT, BxoT = cx.dout("xoT", [D_MODEL, NTOK])
    with cx.st:
        cx.start()
        fw = cx.fw
        cx.make_banks(8)
        R1, _ = cx.sb("R1", [128, NKT * NTOK], F32)
        X, BX = R1[:].rearrange("p (k t) -> p k t", k=NKT), Buf("X")
        R1b = R1[:].bitcast(BF16)
        Y, BY = R1b[:, 0:16384].rearrange("p (k t) -> p k t", k=NKT), Buf("Y")
        yab, Byab = R1b[:, 16384:20480].rearrange("p (k t) -> p k t", k=4), Buf("yab")
        ya32, Bya32 = R1[:, 10240:14336].rearrange("p (k t) -> p k t", k=4), Buf("ya32")
        H, BH = cx.sb("H", [128, NKT, NTOK], BF16)
        MG, BMG = cx.sb("MG", [128, NKT, NTOK], BF16)
        ACTB, BACT = cx.sb("ACTB", [128, 11, NTOK], BF16)
        g, Bg = cx.sb("g", [128, 3 * NKT], F32)
        bs, Bbs = cx.sb("bs", [128, 52], F32)
        cx.sqbufs = [cx.sb(f"sq{i}", [128, 512], F32) for i in range(2)]
        cx.rstd = cx.sb("rstd", [128, 512], F32)
        cx.tmp512 = [cx.sb(f"tmp{i}", [128, 512], F32) for i in range(2)]
        gss = [cx.sb(f"gs{i}", [128, 512], F32) for i in range(3)]
        tts = [cx.sb(f"tt{i}", [128, 512], F32) for i in range(3)]
        ws = WeightStream(cx)
        emit_consts(cx)
        fw.dma("sp", g[:], gn[:, :], Bg, Bgn_d)
        fw.dma("sp", bs[:], bsm[:, :], Bbs, Bbsm_d)
        load_X(cx, X, BX, x1T, Bx1T)
        emit_rmsnorm(cx, X, BX, g[:, 0:NKT], Bg, H, BH)
        fw.barrier()
        fw.dma("sp", ya32, yaT.rearrange("(k p) t -> p k t", p=128), Bya32, ByaT)
        fw.dma("pool", Y[:, 4:12, :], ybT.rearrange("(k p) t -> p k t", p=128), BY, BybT)
        fw.dma("pool", Y[:, 12:16, :], ycT.rearrange("(k p) t -> p k t", p=128), BY, BycT)
        fw.op("dve", lambda v: v.tensor_copy(yab, ya32), reads=[Bya32], writes=[Byab])
        (vg,), Bsg = ws.load([(wglu.rearrange("(kt p) c -> p kt c", p=128), Bwglu, 4, 512)])
        for m in range(4):
            for half in range(2):
                sl = slice(half * 512, (half + 1) * 512)
                pb, Bpb = cx.bank()
                for kt in range(4):
                    mm(fw, pb[:], vg[:, kt, m * 128:(m + 1) * 128], yab[:, kt, sl], kt == 0, kt == 3, [Bsg, Byab], [Bpb])
                sg, Bsgt = cx.tmp512[(m * 2 + half) % 2]
                fw.op("act", lambda a, sg=sg, pb=pb, m=m: a.activation(sg[:], pb[:], AF.Sigmoid, bias=bs[:, m:m + 1]), reads=[Bpb, Bbs], writes=[Bsgt])
                fw.op("dve", lambda v, sg=sg, m=m, sl=sl: v.tensor_tensor(Y[:, m, sl], ya32[:, m, sl], sg[:], ALU.mult), reads=[Bya32, Bsgt], writes=[BY])
        wbrv = wbr.rearrange("(kt p) c -> p kt c", p=128)
        wgv = wg.rearrange("(kt p) c -> p kt c", p=128)
        kranges = [(0, 4), (4, 12), (12, 16)]
        for m in range(NKT):
            cs_ = slice(m * 128, (m + 1) * 128)
            parts = [(wbrv[:, :, cs_], Bwbr, NKT, 128)]
            for i in range(3):
                parts.append((wgv[:, :, i * D_MODEL + m * 128:i * D_MODEL + (m + 1) * 128], Bwg, NKT, 128))
            (vb, v0, v1, v2), Bs = ws.load(parts)
            vgs = [v0, v1, v2]
            for half in range(2):
                sl = slice(half * 512, (half + 1) * 512)
                pbr = []
                for i, (k0, k1) in enumerate(kranges):
                    pb, Bpb = cx.bank()
                    for kt in range(k0, k1):
                        mm(fw, pb[:], vb[:, kt, :], Y[:, kt, sl], kt == k0, kt == k1 - 1, [Bs, BY], [Bpb])
                    pbr.append((pb, Bpb))
                for i in range(3):
                    pb, Bpb = cx.bank()
                    for kt in range(NKT):
                        mm(fw, pb[:], vgs[i][:, kt, :], H[:, kt, sl], kt == 0, kt == NKT - 1, [Bs, BH], [Bpb])
                    gsi, Bgsi = gss[i]
                    fw.op("act", lambda a, gsi=gsi, pb=pb, i=i, m=m: a.activation(gsi[:], pb[:], AF.Sigmoid, bias=bs[:, 4 + i * 16 + m:4 + i * 16 + m + 1]),
                          reads=[Bpb, Bbs], writes=[Bgsi])
                for i in range(3):
                    tti, Btti = tts[i]
                    fw.op("dve", lambda v, tti=tti, i=i, pbr=pbr: v.tensor_tensor(tti[:], pbr[i][0][:], gss[i][0][:], ALU.mult),
                          reads=[pbr[i][1], gss[i][1]], writes=[Btti])
                fw.op("pool", lambda gp: gp.tensor_tensor(tts[0][0][:], tts[0][0][:], tts[1][0][:], ALU.add), reads=[tts[0][1], tts[1][1]], writes=[tts[0][1]])
                fw.op("pool", lambda gp, m=m, sl=sl: gp.tensor_tensor(MG[:, m, sl], tts[0][0][:], tts[2][0][:], ALU.add), reads=[tts[0][1], tts[2][1]], writes=[BMG])
        fw.barrier()
        load_X(cx, X, BX, x1T, Bx1T)
        woutv = wout.rearrange("(kt p) c -> p kt c", p=128)
        for cg in range(8):
            (vw,), Bs = ws.load([(woutv[:, :, cg * 256:(cg + 1) * 256], Bwout, NKT, 256)])
            for gi in range(2):
                m = cg * 2 + gi
                for half in range(2):
                    sl = slice(half * 512, (half + 1) * 512)
                    po, Bpo = cx.bank()
                    for kt in range(NKT):
                        mm(fw, po[:], vw[:, kt, gi * 128:(gi + 1) * 128], MG[:, kt, sl], kt == 0, kt == NKT - 1, [Bs, BMG], [Bpo])
                    fw.op("dve", lambda v, po=po, m=m, sl=sl: v.tensor_tensor(X[:, m, sl], po[:], X[:, m, sl], ALU.add), reads=[Bpo, BX], writes=[BX])
        emit_rmsnorm(cx, X, BX, g[:, NKT:2 * NKT], Bg, H, BH)
        emit_ffn(cx, ws, X, BX, H, BH, w13, Bw13, w2, Bw2, ACTB, BACT)
        if last:
            emit_rmsnorm(cx, X, BX, g[:, 2 * NKT:3 * NKT], Bg, X, BX)
        store_X(cx, X, BX, xoT, BxoT)
        fw.final_wait("sp", cx.out_bufs)
    return cx.nc


def run_C(inputs, l, x1T_shards, ya, ybb, yc, last):
    nc = _get("C1" if last else "C0", lambda: build_C(last))
    nw = inputs["norm_w"][l]
    gn = np.ascontiguousarray(np.concatenate([nw[1].reshape(NKT, 128).T, nw[2].reshape(NKT, 128).T,
                                              inputs["final_norm"].reshape(NKT, 128).T], axis=1))
    bsm = np.ascontiguousarray(np.concatenate([inputs["s5_b_glu"][l].reshape(4, 128).T,
                                               inputs["b_gate"][l].reshape(48, 128).T], axis=1))
    in_maps = []
    for c in range(8):
        b, j = c // 4, c % 4
        ts_ = slice(j * NTOK, (j + 1) * NTOK)
        in_maps.append({"x1T": x1T_shards[c],
                        "yaT": np.ascontiguousarray(ya[b][ts_].T), "ybT": np.ascontiguousarray(ybb[b][ts_].T),
                        "ycT": np.ascontiguousarray(yc[b][ts_].T), "gn": gn, "bsm": bsm,
                        "wglu": np.ascontiguousarray(inputs["s5_w_glu"][l]), "wbr": np.ascontiguousarray(inputs["w_branch"][l]),
                        "wg": np.ascontiguousarray(inputs["w_gate"][l]), "wout": np.ascontiguousarray(inputs["w_out"][l]),
                        "w13": np.ascontiguousarray(inputs["ffn_w13"][l, 1]), "w2": np.ascontiguousarray(inputs["ffn_w2"][l, 1])})
    res = _run(nc, in_maps)
    return [res[c]["xoT"] for c in range(8)]


def kernel(**inputs):
    inputs = {k: np.asarray(v) for k, v in inputs.items()}
    x = inputs["x"].reshape(BATCH * SEQ, D_MODEL)
    xT = [np.ascontiguousarray(x[c * NTOK:(c + 1) * NTOK].T) for c in range(8)]
    for l in range(DEPTH):
        resA = run_A(xT, inputs, l)
        x1T = [resA[c]["x1T"] for c in range(8)]

        def cat(name, rows):
            return [np.concatenate([resA[b * 4 + j][name][rows].T for j in range(4)], axis=0) for b in range(BATCH)]
        q_b = cat("qkT", slice(0, 1024)); k_b = cat("qkT", slice(1024, 2048)); v_b = cat("vT", slice(0, 1024))
        u_b = cat("restT", slice(0, 512)); bg_b = cat("restT", slice(512, 1024))
        cg_b = cat("restT", slice(1024, 1536)); xv_b = cat("restT", slice(1536, 2048))
        ya, ybb, yc = run_M(inputs, l, q_b, k_b, v_b, u_b, bg_b, cg_b, xv_b)
        xT = run_C(inputs, l, x1T, ya, ybb, yc, last=(l == DEPTH - 1))
    out = np.concatenate([xT[c].T for c in range(8)], axis=0).reshape(BATCH, SEQ, D_MODEL)
    return np.ascontiguousarray(out.astype(np.float32))
```

```python
import math
import bisect
from contextlib import ExitStack
import numpy as np
import ml_dtypes
import concourse.bass as bass
import concourse.mybir as mybir
from concourse.bass_utils import run_bass_kernel_spmd

F32 = mybir.dt.float32
BF16 = mybir.dt.bfloat16
AF = mybir.ActivationFunctionType
ALU = mybir.AluOpType
AX = mybir.AxisListType

D_MODEL = 2048
SEQ = 4096
BATCH = 2
DEPTH = 2
D_FF = 5504
D_IN = 5120
NTOK = 1024
NKT = 16
NFT = 43
EPS = 1e-6
PI = math.pi
TWO_PI = 2.0 * math.pi


class Ticket:
    __slots__ = ("eng", "idx", "sem", "value")

    def __init__(self, eng, idx, sem, value):
        self.eng, self.idx, self.sem, self.value = eng, idx, sem, value


class Buf:
    def __init__(self, name, excl=False):
        self.name = name
        self.writer = None
        self.readers = {}
        self.dma_sem = None
        self.dma_cnt = 0
        self.excl = excl


class Eng:
    def __init__(self, name, handle, sem):
        self.name, self.h, self.sem = name, handle, sem
        self.count = 0
        self.n = 0
        self.last = None
        self.last_sig = True
        self.sig_idx = []
        self.sig_val = []
        self.known = {}

    def materialize(self, t):
        if t.value is not None:
            return
        i = bisect.bisect_left(self.sig_idx, t.idx)
        if i < len(self.sig_idx):
            t.value = self.sig_val[i]
            return
        assert not self.last_sig
        self.count += 1
        self.last.then_inc(self.sem, 1)
        self.last_sig = True
        self.sig_idx.append(self.n - 1)
        self.sig_val.append(self.count)
        t.value = self.count


class FW:
    def __init__(self, nc, st):
        self.nc = nc
        self.st = st
        self.engs = {}
        self.dma_bufs = []
        for n, h in (("pe", nc.tensor), ("act", nc.scalar), ("dve", nc.vector), ("pool", nc.gpsimd), ("sp", nc.sync)):
            self.engs[n] = Eng(n, h, self.new_sem("e_" + n))

    def new_sem(self, name):
        return self.st.enter_context(self.nc.semaphore(name))

    def _wait(self, e, t):
        if t.eng is not None:
            if t.eng is e and e.name == "pe":
                return
            t.eng.materialize(t)
        key = id(t.sem)
        if e.known.get(key, 0) >= t.value:
            return
        e.h.wait_ge(t.sem, t.value)
        e.known[key] = t.value

    def _deps(self, e, reads, writes):
        for b in reads:
            if b.writer is not None:
                self._wait(e, b.writer)
        for b in writes:
            if b.writer is not None:
                self._wait(e, b.writer)
            for r in b.readers.values():
                self._wait(e, r)

    def _track(self, t, key, reads, writes):
        for b in reads:
            b.readers[key] = t
        for b in writes:
            b.writer = t
            b.readers = {}

    def op(self, ename, fn, reads=(), writes=(), signal=True):
        e = self.engs[ename]
        rd = [b for b in reads if not b.excl]
        wr = list(writes) + [b for b in reads if b.excl]
        self._deps(e, rd, wr)
        ins = fn(e.h)
        idx = e.n
        e.n += 1
        e.last = ins
        if signal:
            e.count += 1
            ins.then_inc(e.sem, 1)
            e.sig_idx.append(idx)
            e.sig_val.append(e.count)
            e.last_sig = True
            t = Ticket(e, idx, e.sem, e.count)
        else:
            e.last_sig = False
            t = Ticket(e, idx, e.sem, None)
        self._track(t, ename, rd, wr)
        return ins

    def dma(self, ename, out, in_, dst, src, **kw):
        e = self.engs[ename]
        self._deps(e, [src], [dst])
        if dst.dma_sem is None:
            dst.dma_sem = self.new_sem("d_" + dst.name)
            self.dma_bufs.append(dst)
        dst.dma_cnt += 16
        ins = e.h.dma_start(out=out, in_=in_, **kw)
        ins.then_inc(dst.dma_sem, 16)
        t = Ticket(None, -1, dst.dma_sem, dst.dma_cnt)
        self._track(t, "dma_" + dst.name, [src], [dst])
        return ins

    def barrier(self):
        ts = []
        for e2 in self.engs.values():
            if e2.n > 0:
                ts.append(Ticket(e2, e2.n - 1, e2.sem, None))
        for b in self.dma_bufs:
            ts.append(Ticket(None, -1, b.dma_sem, b.dma_cnt))
        for e in self.engs.values():
            for t in ts:
                if t.eng is e:
                    continue
                self._wait(e, t)

    def final_wait(self, ename, bufs):
        e = self.engs[ename]
        for b in bufs:
            if b.writer is not None:
                self._wait(e, b.writer)


class Ctx:
    def __init__(self, name):
        self.nc = bass.Bass("TRN2", target_bir_lowering=False)
        self.st = ExitStack()
        self.fw = None
        self.name = name
        self.cnt = 0
        self.out_bufs = []
        self.banks = []
        self.bank_i = 0

    def start(self):
        self.fw = FW(self.nc, self.st)

    def din(self, name, shape, dt=F32):
        return self.nc.dram_tensor(name, list(shape), dt, kind="ExternalInput").ap(), Buf("i_" + name)

    def dout(self, name, shape, dt=F32):
        b = Buf("o_" + name)
        self.out_bufs.append(b)
        return self.nc.dram_tensor(name, list(shape), dt, kind="ExternalOutput").ap(), b

    def sb(self, name, shape, dt=F32):
        return self.st.enter_context(self.nc.sbuf_tensor(name, list(shape), dt)), Buf(name)

    def ps(self, name, shape, dt=F32):
        return self.st.enter_context(self.nc.psum_tensor(name, list(shape), dt)), Buf(name, excl=True)

    def make_banks(self, n):
        for i in range(n):
            self.banks.append(self.ps(f"bank{i}", [128, 512], F32))

    def bank(self):
        t = self.banks[self.bank_i % len(self.banks)]
        self.bank_i += 1
        return t


def mm(fw, out, lhsT, rhs, start, stop, reads, writes):
    return fw.op("pe", lambda p: p.matmul(out, lhsT=lhsT, rhs=rhs, start=start, stop=stop),
                 reads=reads, writes=writes, signal=stop)


class WeightStream:
    def __init__(self, cx, nslots=2, slot_elems=8192):
        self.cx = cx
        self.slots = [cx.sb(f"wslot{i}", [128, slot_elems], BF16) for i in range(nslots)]
        self.i = 0

    def load(self, parts):
        t, B = self.slots[self.i % len(self.slots)]
        self.i += 1
        views = []
        off = 0
        for (dap, dbuf, nk, ncols) in parts:
            v = t[:, off:off + nk * ncols].rearrange("p (k c) -> p k c", c=ncols)
            self.cx.fw.dma("pool", v, dap, B, dbuf)
            views.append(v)
            off += nk * ncols
        assert off <= 8192
        return views, B


MAGIC = 12582912.0
C1_2PI = 6.28125
C2_2PI = TWO_PI - 6.28125


def emit_round(fw, eng, out, Bout, in_, Bin, scale, offset):
    if offset != 0.0:
        fw.op(eng, lambda v: v.tensor_scalar(out, in_, scale, offset, ALU.mult, ALU.add), reads=[Bin], writes=[Bout])
        fw.op(eng, lambda v: v.tensor_scalar_add(out, out, MAGIC), reads=[Bout], writes=[Bout])
    else:
        fw.op(eng, lambda v: v.tensor_scalar(out, in_, scale, MAGIC, ALU.mult, ALU.add), reads=[Bin], writes=[Bout])
    fw.op(eng, lambda v: v.tensor_scalar_add(out, out, -MAGIC), reads=[Bout], writes=[Bout])


def emit_sin(fw, out, Bout, x, Bx, phase, k, Bk):
    if phase != 0.0:
        fw.op("dve", lambda v: v.tensor_scalar_add(out, x, phase), reads=[Bx], writes=[Bout])
        x, Bx = out, Bout
    emit_round(fw, "dve", k, Bk, x, Bx, 1.0 / TWO_PI, 0.0)
    fw.op("dve", lambda v: v.scalar_tensor_tensor(out, k, -C1_2PI, x, ALU.mult, ALU.add), reads=[Bk, Bx], writes=[Bout])
    fw.op("dve", lambda v: v.scalar_tensor_tensor(out, k, -C2_2PI, out, ALU.mult, ALU.add), reads=[Bk, Bout], writes=[Bout])
    fw.op("dve", lambda v: v.tensor_scalar(out, out, -PI, PI, ALU.max, ALU.min), reads=[Bout], writes=[Bout])
    fw.op("act", lambda a: a.activation(out, out, AF.Sin), reads=[Bout], writes=[Bout])


def emit_consts(cx):
    fw = cx.fw
    ones, Bones = cx.sb("ones", [128, 128], F32)
    fw.op("dve", lambda v: v.memset(ones[:], 1.0), writes=[Bones])
    cx.ones, cx.Bones = ones, Bones
    epsc, Bepsc = cx.sb("epsc", [128, 1], F32)
    fw.op("dve", lambda v: v.memset(epsc[:], EPS), writes=[Bepsc])
    cx.epsc, cx.Bepsc = epsc, Bepsc


def emit_rmsnorm(cx, X, BX, g, Bg, H, BH):
    fw = cx.fw
    for half in range(2):
        sl = slice(half * 512, (half + 1) * 512)
        pb, Bpb = cx.bank()
        for kt in range(NKT):
            sq, Bsq = cx.sqbufs[kt % 2]
            fw.op("act", lambda a, sq=sq, kt=kt: a.activation(sq[:], X[:, kt, sl], AF.Square), reads=[BX], writes=[Bsq])
            mm(fw, pb[:], cx.ones[:], sq[:], kt == 0, kt == NKT - 1, [cx.Bones, Bsq], [Bpb])
        rstd, Brstd = cx.rstd
        fw.op("act", lambda a: a.activation(rstd[:], pb[:], AF.Sqrt, bias=cx.epsc[:, 0:1], scale=1.0 / D_MODEL), reads=[Bpb, cx.Bepsc], writes=[Brstd])
        fw.op("dve", lambda v: v.reciprocal(rstd[:], rstd[:]), reads=[Brstd], writes=[Brstd])
        for kt in range(NKT):
            eng = "dve"
            fw.op(eng, lambda v, kt=kt: v.scalar_tensor_tensor(H[:, kt, sl], X[:, kt, sl], g[:, kt:kt + 1], rstd[:], ALU.mult, ALU.mult),
                  reads=[BX, Bg, Brstd], writes=[BH])


def emit_ffn(cx, ws, X, BX, H, BH, w13, Bw13, w2, Bw2, ACTB, BACT):
    fw = cx.fw
    w13v = w13.rearrange("(kt p) c -> p kt c", p=128)
    quarters = [(0, 11), (11, 22), (22, 33), (33, 43)]
    for (f0, f1) in quarters:
        nft = f1 - f0
        ft = f0
        while ft < f1:
            ng = min(2, f1 - ft)
            nc_ = ng * 128
            (va, vb), Bs = ws.load([(w13v[:, :, ft * 128:ft * 128 + nc_], Bw13, NKT, nc_),
                                    (w13v[:, :, D_FF + ft * 128:D_FF + ft * 128 + nc_], Bw13, NKT, nc_)])
            for gi in range(ng):
                for half in range(2):
                    sl = slice(half * 512, (half + 1) * 512)
                    pa, Bpa = cx.bank()
                    pbb, Bpbb = cx.bank()
                    for kt in range(NKT):
                        mm(fw, pa[:], va[:, kt, gi * 128:(gi + 1) * 128], H[:, kt, sl], kt == 0, kt == NKT - 1, [Bs, BH], [Bpa])
                    for kt in range(NKT):
                        mm(fw, pbb[:], vb[:, kt, gi * 128:(gi + 1) * 128], H[:, kt, sl], kt == 0, kt == NKT - 1, [Bs, BH], [Bpbb])
                    sa, Bsa = cx.tmp512[cx.cnt % 2]
                    cx.cnt += 1
                    fw.op("act", lambda a, sa=sa, pa=pa: a.activation(sa[:], pa[:], AF.Silu), reads=[Bpa], writes=[Bsa])
                    fl = ft + gi - f0
                    fw.op("dve", lambda v, sa=sa, pbb=pbb, fl=fl: v.tensor_tensor(ACTB[:, fl, sl], sa[:], pbb[:], ALU.mult),
                          reads=[Bsa, Bpbb], writes=[BACT])
            ft += ng
        w2v = w2[f0 * 128:f1 * 128, :].rearrange("(kt p) c -> p kt c", p=128)
        for cg in range(8):
            (vw,), Bs = ws.load([(w2v[:, :, cg * 256:(cg + 1) * 256], Bw2, nft, 256)])
            for gi in range(2):
                m = cg * 2 + gi
                for half in range(2):
                    sl = slice(half * 512, (half + 1) * 512)
                    po, Bpo = cx.bank()
                    for kt in range(nft):
                        mm(fw, po[:], vw[:, kt, gi * 128:(gi + 1) * 128], ACTB[:, kt, sl], kt == 0, kt == nft - 1, [Bs, BACT], [Bpo])
                    fw.op("dve", lambda v, po=po, m=m: v.scalar_tensor_tensor(X[:, m, sl], po[:], 0.5, X[:, m, sl], ALU.mult, ALU.add),
                          reads=[Bpo, BX], writes=[BX])


def load_X(cx, X, BX, xT, BxT):
    xv = xT.rearrange("(kt p) t -> p kt t", p=128)
    for kt in range(0, NKT, 4):
        cx.fw.dma("sp", X[:, kt:kt + 4, :], xv[:, kt:kt + 4, :], BX, BxT)


def store_X(cx, X, BX, xT, BxT):
    xv = xT.rearrange("(kt p) t -> p kt t", p=128)
    for kt in range(0, NKT, 4):
        cx.fw.dma("sp", xv[:, kt:kt + 4, :], X[:, kt:kt + 4, :], BxT, BX)


def build_A():
    cx = Ctx("A")
    nc = cx.nc
    xT, BxT = cx.din("xT", [D_MODEL, NTOK])
    gn, Bgn_d = cx.din("gn", [128, 2 * NKT])
    pos, Bpos_d = cx.din("pos", [128, NTOK])
    w13, Bw13 = cx.din("w13", [D_MODEL, 2 * D_FF])
    w2, Bw2 = cx.din("w2", [D_FF, D_MODEL])
    win, Bwin = cx.din("win", [D_MODEL, D_IN])
    x1T, Bx1T = cx.dout("x1T", [D_MODEL, NTOK])
    qkT, BqkT = cx.dout("qkT", [2048, NTOK], BF16)
    vT, BvT = cx.dout("vT", [1024, NTOK], BF16)
    restT, BrestT = cx.dout("restT", [2048, NTOK])
    with cx.st:
        cx.start()
        fw = cx.fw
        cx.make_banks(8)
        X, BX = cx.sb("X", [128, NKT, NTOK], F32)
        H, BH = cx.sb("H", [128, NKT, NTOK], BF16)
        ACTB, BACT = cx.sb("ACTB", [128, 11, NTOK], BF16)
        g, Bg = cx.sb("g", [128, 2 * NKT], F32)
        cx.sqbufs = [cx.sb(f"sq{i}", [128, 512], F32) for i in range(2)]
        cx.rstd = cx.sb("rstd", [128, 512], F32)
        cx.tmp512 = [cx.sb(f"tmp{i}", [128, 512], F32) for i in range(2)]
        ws = WeightStream(cx)
        emit_consts(cx)
        fw.dma("sp", g[:], gn[:, :], Bg, Bgn_d)
        load_X(cx, X, BX, xT, BxT)
        pidx, Bpidx = cx.sb("pidx", [128, 1], F32)
        fw.op("pool", lambda gp: gp.iota(pidx[:], pattern=[[0, 1]], base=0, channel_multiplier=1,
                                         allow_small_or_imprecise_dtypes=True), writes=[Bpidx])
        invf, Binvf = cx.sb("invf", [128, 1], F32)
        kk, Bkk = cx.sb("kk", [128, 1], F32)
        emit_round(fw, "dve", kk[:], Bkk, pidx[:], Bpidx, 1.0 / 32.0, -0.484375)
        fw.op("dve", lambda v: v.scalar_tensor_tensor(invf[:], kk[:], -32.0, pidx[:], ALU.mult, ALU.add), reads=[Bkk, Bpidx], writes=[Binvf])
        fw.op("act", lambda a: a.activation(invf[:], invf[:], AF.Exp, scale=-math.log(10000.0) / 32.0), reads=[Binvf], writes=[Binvf])
        mlow, Bmlow = cx.sb("mlow", [128, 1], F32)
        emit_round(fw, "dve", kk[:], Bkk, pidx[:], Bpidx, 1.0 / 64.0, -0.4921875)
        fw.op("dve", lambda v: v.scalar_tensor_tensor(mlow[:], kk[:], -64.0, pidx[:], ALU.mult, ALU.add), reads=[Bkk, Bpidx], writes=[Bmlow])
        fw.op("dve", lambda v: v.tensor_single_scalar(mlow[:], mlow[:], 32.0, ALU.is_lt), reads=[Bmlow], writes=[Bmlow])
        mhigh, Bmhigh = cx.sb("mhigh", [128, 1], F32)
        fw.op("dve", lambda v: v.tensor_scalar(mhigh[:], mlow[:], -1.0, 1.0, ALU.mult, ALU.add), reads=[Bmlow], writes=[Bmhigh])
        sgn, Bsgn = cx.sb("sgn", [128, 1], F32)
        fw.op("dve", lambda v: v.tensor_scalar(sgn[:], mlow[:], -2.0, 1.0, ALU.mult, ALU.add), reads=[Bmlow], writes=[Bsgn])
        Ct, BCt = cx.sb("Ct", [128, NTOK], F32)
        St, BSt = cx.sb("St", [128, NTOK], F32)
        ang, Bang = cx.sb("ang", [128, NTOK], F32)
        kt_, Bkt_ = cx.sb("ktmp", [128, NTOK], F32)
        fw.dma("sp", St[:], pos[:, :], BSt, Bpos_d)
        fw.op("dve", lambda v: v.tensor_scalar(ang[:], St[:], invf[:, 0:1], None, ALU.mult), reads=[BSt, Binvf], writes=[Bang])
        emit_sin(fw, St[:], BSt, ang[:], Bang, 0.0, kt_[:], Bkt_)
        fw.op("dve", lambda v: v.tensor_scalar(St[:], St[:], sgn[:, 0:1], None, ALU.mult), reads=[BSt, Bsgn], writes=[BSt])
        emit_sin(fw, Ct[:], BCt, ang[:], Bang, 0.5 * PI, kt_[:], Bkt_)
        dmat, Bdmat = cx.sb("dmat", [128, 128], F32)
        fw.op("pool", lambda gp: gp.iota(dmat[:], pattern=[[1, 128]], base=0, channel_multiplier=-1,
                                         allow_small_or_imprecise_dtypes=True), writes=[Bdmat])
        Pm, BPm = cx.sb("Pm", [128, 128], F32)
        e2, Be2 = cx.sb("e2", [128, 128], F32)
        fw.op("dve", lambda v: v.tensor_scalar(Pm[:], dmat[:], 32.0, mlow[:, 0:1], ALU.is_equal, ALU.mult), reads=[Bdmat, Bmlow], writes=[BPm])
        fw.op("dve", lambda v: v.tensor_scalar(e2[:], dmat[:], -32.0, mhigh[:, 0:1], ALU.is_equal, ALU.mult), reads=[Bdmat, Bmhigh], writes=[Be2])
        fw.op("dve", lambda v: v.tensor_tensor(Pm[:], Pm[:], e2[:], ALU.add), reads=[BPm, Be2], writes=[BPm])

        emit_rmsnorm(cx, X, BX, g[:, 0:NKT], Bg, H, BH)
        emit_ffn(cx, ws, X, BX, H, BH, w13, Bw13, w2, Bw2, ACTB, BACT)
        store_X(cx, X, BX, x1T, Bx1T)
        emit_rmsnorm(cx, X, BX, g[:, NKT:2 * NKT], Bg, H, BH)
        winv = win.rearrange("(kt p) c -> p kt c", p=128)
        stg32 = [cx.sb(f"stg32_{i}", [128, 512], F32) for i in range(3)]
        stg16 = [cx.sb(f"stg16_{i}", [128, 512], BF16) for i in range(3)]
        t1b = [cx.sb(f"t1b_{i}", [128, 512], F32) for i in range(2)]
        si = 0
        for cg in range(20):
            (vw,), Bs = ws.load([(winv[:, :, cg * 256:(cg + 1) * 256], Bwin, NKT, 256)])
            for gi in range(2):
                ct = cg * 2 + gi
                for half in range(2):
                    sl = slice(half * 512, (half + 1) * 512)
                    pb, Bpb = cx.bank()
                    for kt in range(NKT):
                        mm(fw, pb[:], vw[:, kt, gi * 128:(gi + 1) * 128], H[:, kt, sl], kt == 0, kt == NKT - 1, [Bs, BH], [Bpb])
                    si += 1
                    if ct < 4 or ct >= 28:
                        s32, Bs32 = stg32[si % 3]
                        fw.op("act", lambda a, s32=s32, pb=pb: a.copy(s32[:], pb[:]), reads=[Bpb], writes=[Bs32])
                        row = ct * 128 if ct < 4 else (ct - 28 + 4) * 128
                        fw.dma("sp", restT[row:row + 128, sl], s32[:], BrestT, Bs32)
                    elif ct < 20:
                        s32, Bs32 = stg32[si % 3]
                        fw.op("act", lambda a, s32=s32, pb=pb: a.copy(s32[:], pb[:]), reads=[Bpb], writes=[Bs32])
                        pp, Bpp = cx.bank()
                        mm(fw, pp[:], Pm[:], s32[:], True, True, [BPm, Bs32], [Bpp])
                        ta, Bta = t1b[si % 2]
                        fw.op("pool", lambda gp, ta=ta, s32=s32: gp.tensor_tensor(ta[:], s32[:], Ct[:, sl], ALU.mult), reads=[Bs32, BCt], writes=[Bta])
                        tb, Btb = cx.tmp512[si % 2]
                        fw.op("dve", lambda v, tb=tb, pp=pp: v.tensor_tensor(tb[:], pp[:], St[:, sl], ALU.mult), reads=[Bpp, BSt], writes=[Btb])
                        s16, Bs16 = stg16[si % 3]
                        fw.op("dve", lambda v, s16=s16, ta=ta, tb=tb: v.tensor_tensor(s16[:], ta[:], tb[:], ALU.add), reads=[Bta, Btb], writes=[Bs16])
                        row = (ct - 4) * 128
                        fw.dma("sp", qkT[row:row + 128, sl], s16[:], BqkT, Bs16)
                    else:
                        s16, Bs16 = stg16[si % 3]
                        fw.op("act", lambda a, s16=s16, pb=pb: a.copy(s16[:], pb[:]), reads=[Bpb], writes=[Bs16])
                        row = (ct - 20) * 128
                        fw.dma("sp", vT[row:row + 128, sl], s16[:], BvT, Bs16)
        fw.final_wait("sp", cx.out_bufs)
    return nc


_CACHE = {}


def _get(name, builder):
    if name not in _CACHE:
        _CACHE[name] = builder()
    return _CACHE[name]


TRACE = False
LAST_NS = [None]


def _run(nc, in_maps):
    if TRACE:
        res = run_bass_kernel_spmd(nc, in_maps, core_ids=list(range(8)), trace=True)
        LAST_NS[0] = res.exec_time_ns
    else:
        res = run_bass_kernel_spmd(nc, in_maps, core_ids=list(range(8)))
    return res.results


def _gains(norm_w_l, idxs):
    return np.ascontiguousarray(np.concatenate([norm_w_l[i].reshape(NKT, 128).T for i in idxs], axis=1))


def run_A(xT_shards, inputs, l):
    nc = _get("A", build_A)
    gn = _gains(inputs["norm_w"][l], [0, 1])
    in_maps = []
    for c in range(8):
        p0 = (c % 4) * NTOK
        pos = np.ascontiguousarray(np.broadcast_to(np.arange(p0, p0 + NTOK, dtype=np.float32)[None, :], (128, NTOK)))
        in_maps.append({"xT": xT_shards[c], "gn": gn, "pos": pos,
                        "w13": np.ascontiguousarray(inputs["ffn_w13"][l, 0]), "w2": np.ascontiguousarray(inputs["ffn_w2"][l, 0]),
                        "win": np.ascontiguousarray(inputs["w_in"][l])})
    return _run(nc, in_maps)


TS = 256
NCH = SEQ // TS


def rev_ap(a):
    n = a.ap[-1][1]
    st = a.ap[-1][0]
    return bass.AP(tensor=a.tensor, offset=a.offset + (n - 1) * st, ap=[list(x) for x in a.ap[:-1]] + [[-st, n]])


def build_M():
    cx = Ctx("M")
    qT, BqT = cx.din("qT", [256, SEQ], BF16)
    kT, BkT = cx.din("kT", [256, SEQ], BF16)
    vv, Bvv = cx.din("v", [SEQ, 256], BF16)
    uT, BuT = cx.din("uT", [128, SEQ])
    bgT, BbgT = cx.din("bgT", [128, SEQ])
    cgT, BcgT = cx.din("cgT", [128, SEQ])
    xvT, BxvT = cx.din("xvT", [128, SEQ])
    lamc, Blamc = cx.din("lamc", [128, 24])
    lamr, Blamr = cx.din("lamr", [128, 2 * 3 * 512])
    BTd, BBTd = cx.din("BT", [128, 2 * 512])
    CTd, BCTd = cx.din("CT", [128, 2 * 4 * 128])
    smalls, Bsmalls = cx.din("smalls", [128, 1 + 3 + 2])
    dl, Bdl = cx.din("dl", [128, 256])
    sub, Bsub = cx.din("subln", [128, 128])
    yaT, ByaT = cx.dout("yaT", [128, SEQ])
    yb, Byb = cx.dout("yb", [SEQ, 256])
    ycT, BycT = cx.dout("ycT", [128, SEQ])
    with cx.st:
        cx.start()
        fw = cx.fw
        BP = Buf("params")
        sm, Bsm = cx.sb("sm", [128, 6])
        fw.dma("sp", sm[:], smalls[:, :], Bsm, Bsmalls)
        epsc, Bepsc = cx.sb("epsc", [128, 1])
        fw.op("dve", lambda v: v.memset(epsc[:], EPS), writes=[Bepsc])

        with ExitStack() as tst:
            def tsb(name, shape, dt=F32):
                return tst.enter_context(cx.nc.sbuf_tensor(name, list(shape), dt)), Buf(name)
            cg, Bcg = tsb("cg", [128, SEQ])
            xv, Bxv = tsb("xv", [128, SEQ])
            bg, Bbg = tsb("bg", [128, SEQ])
            fw.dma("sp", cg[:], cgT[:, :], Bcg, BcgT)
            fw.dma("sp", xv[:], xvT[:, :], Bxv, BxvT)
            fw.dma("sp", bg[:], bgT[:, :], Bbg, BbgT)
            fw.op("pool", lambda g: g.tensor_tensor(cg[:], cg[:], xv[:], ALU.mult), reads=[Bcg, Bxv], writes=[Bcg])
            fw.op("dve", lambda v: v.tensor_scalar(xv[:], cg[:], sm[:, 2:3], None, ALU.mult), reads=[Bcg, Bsm], writes=[Bxv])
            fw.op("dve", lambda v: v.scalar_tensor_tensor(xv[:, 1:SEQ], cg[:, 0:SEQ - 1], sm[:, 1:2], xv[:, 1:SEQ], ALU.mult, ALU.add),
                  reads=[Bcg, Bsm, Bxv], writes=[Bxv])
            fw.op("dve", lambda v: v.scalar_tensor_tensor(xv[:, 0:SEQ - 1], cg[:, 1:SEQ], sm[:, 3:4], xv[:, 0:SEQ - 1], ALU.mult, ALU.add),
                  reads=[Bcg, Bsm, Bxv], writes=[Bxv])
            fw.op("pool", lambda g: g.tensor_tensor(bg[:], bg[:], xv[:], ALU.mult), reads=[Bbg, Bxv], writes=[Bbg])
            fw.dma("sp", ycT[:, :], bg[:], BycT, Bbg)
            fw.barrier()

        cx.banks = []
        accs = [cx.ps(f"acc{i}", [128, 512], F32) for i in range(2)]
        sbanks = [cx.ps(f"sbank{i}", [128, 512], F32) for i in range(3)]
        psBU, BpsBU = cx.ps("psBU", [128, 2, 512], F32)
        psY, BpsY = cx.ps("psY", [128, 512], F32)
        qs, Bqs = cx.sb("qs", [128, 2, SEQ], BF16)
        ks, Bks = cx.sb("ks", [128, 2, SEQ], BF16)
        Vaug, BVaug = cx.sb("Vaug", [128, 2, 32, 130], BF16)
        u, Bu = cx.sb("u", [128, SEQ])
        yacc, Byacc = cx.sb("yacc", [128, SEQ])
        fw.dma("sp", qs[:], qT.rearrange("(h p) t -> p h t", p=128), Bqs, BqT)
        fw.dma("sp", ks[:], kT.rearrange("(h p) t -> p h t", p=128), Bks, BkT)
        fw.op("pool", lambda g: g.memset(Vaug[:], 1.0), writes=[BVaug])
        for hh in range(2):
            vsrc = vv[:, hh * 128:(hh + 1) * 128].rearrange("(kt p) d -> p kt d", p=128)
            for k8 in range(4):
                fw.dma("sp", Vaug[:, hh, k8 * 8:(k8 + 1) * 8, 0:128], vsrc[:, k8 * 8:(k8 + 1) * 8, :], BVaug, Bvv)
        fw.dma("sp", u[:], uT[:, :], Bu, BuT)

        dls, Bdls = cx.sb("dls", [128, 256])
        fw.dma("sp", dls[:], dl[:, :], Bdls, Bdl)
        gsc, Bgsc = cx.sb("gsc", [128, 128])
        fw.dma("sp", gsc[:], sub[:, :], Bgsc, Bsub)
        fw.op("dve", lambda v: v.tensor_scalar(gsc[:], gsc[:], sm[:, 5:6], None, ALU.mult), reads=[Bgsc, Bsm], writes=[Bgsc])
        pr, Bpr = cx.sb("pr", [128, 128])
        s2, Bs2 = cx.sb("s2", [128, 2])
        fw.op("dve", lambda v: v.tensor_tensor(pr[:, 0:64], dls[:, 0:64], dls[:, 64:128], ALU.mult), reads=[Bdls], writes=[Bpr])
        fw.op("dve", lambda v: v.tensor_tensor(pr[:, 64:128], dls[:, 128:192], dls[:, 192:256], ALU.mult), reads=[Bdls, Bpr], writes=[Bpr])
        fw.op("dve", lambda v: v.reduce_sum(s2[:, 0:1], pr[:, 0:64], axis=AX.X), reads=[Bpr], writes=[Bs2])
        fw.op("dve", lambda v: v.reduce_sum(s2[:, 1:2], pr[:, 64:128], axis=AX.X), reads=[Bpr, Bs2], writes=[Bs2])
        fw.op("act", lambda a: a.activation(s2[:], s2[:], AF.Exp), reads=[Bs2], writes=[Bs2])
        neglam, Bneglam = cx.sb("neglam", [128, 1])
        fw.op("dve", lambda v: v.tensor_tensor(neglam[:], s2[:, 1:2], s2[:, 0:1], ALU.subtract), reads=[Bs2], writes=[Bneglam])
        fw.op("dve", lambda v: v.tensor_tensor(neglam[:], neglam[:], sm[:, 4:5], ALU.subtract), reads=[Bneglam, Bsm], writes=[Bneglam])

        lc, _ = cx.sb("lc", [128, 24])
        big, _ = cx.sb("big", [128, 8192])
        lr_ = big[:, 0:3072]
        BT, _ = cx.sb("BTs", [128, 1024])
        CT, _ = cx.sb("CTs", [128, 1024])
        fw.dma("sp", lc[:], lamc[:, :], BP, Blamc)
        fw.dma("sp", lr_, lamr[:, :], BP, Blamr)
        fw.dma("sp", BT[:], BTd[:, :], BP, BBTd)
        fw.dma("sp", CT[:], CTd[:, :], BP, BCTd)
        CTv = CT[:].rearrange("p (c s h) -> p c s h", c=2, s=4)

        def P(eng, f):
            fw.op(eng, f, reads=[BP], writes=[BP])
        P("dve", lambda v: v.tensor_scalar(CT[:, 512:1024], CT[:, 512:1024], -1.0, None, ALU.mult))
        class _V2:
            def __init__(self, ap):
                self.ap = ap

            def __getitem__(self, k):
                if isinstance(k, tuple):
                    return self.ap[k]
                return self.ap
        Jf = _V2(big[:, 6656:6912])
        Jb = _V2(big[:, 6912:7168])
        P("pool", lambda g: g.iota(Jf[:], pattern=[[1, TS]], base=0, channel_multiplier=0, allow_small_or_imprecise_dtypes=True))
        P("pool", lambda g: g.iota(Jb[:], pattern=[[-1, TS]], base=TS - 1, channel_multiplier=0, allow_small_or_imprecise_dtypes=True))
        mf = _V2(big[:, 7168:7424])
        mb = _V2(big[:, 7424:7680])
        P("dve", lambda v: v.memset(mf[:], 1.0))
        P("dve", lambda v: v.memset(mf[:, 0:1], 0.0))
        P("dve", lambda v: v.memset(mb[:], 1.0))
        P("dve", lambda v: v.memset(mb[:, TS - 1:TS], 0.0))
        cosT = [cx.sb(f"cosT{d}", [128, 4, TS])[0] for d in range(2)]
        sinT = [cx.sb(f"sinT{d}", [128, 4, TS])[0] for d in range(2)]
        rmask = [cx.sb(f"rmask{d}", [128, 4, TS])[0] for d in range(2)]
        G = [cx.sb(f"G{d}", [128, 2, 4])[0] for d in range(2)]
        BbT = [cx.sb(f"BbT{d}", [128, 2, 512])[0] for d in range(2)]
        dtc, _ = cx.sb("dtc", [128, 4])
        thc, _ = cx.sb("thc", [128, 4])
        rhoc, _ = cx.sb("rhoc", [128, 4])
        angc, _ = cx.sb("angc", [128, 4])
        kc, _ = cx.sb("kc", [128, 4])
        ang, _ = cx.sb("angt", [128, TS])
        kt_, _ = cx.sb("ktt", [128, TS])
        class _V:
            def __init__(self, ap):
                self.ap = ap

            def __getitem__(self, k):
                return self.ap
        r1, r2, r3, r4, r5, r6, kr = [_V(big[:, 3072 + i * 512:3072 + (i + 1) * 512]) for i in range(7)]
        for d in range(2):
            lrc = lc[:, d * 12 + 0:d * 12 + 4]
            lic = lc[:, d * 12 + 4:d * 12 + 8]
            ldc = lc[:, d * 12 + 8:d * 12 + 12]
            P("act", lambda a: a.activation(dtc[:], ldc, AF.Exp))
            P("dve", lambda v: v.tensor_tensor(thc[:], dtc[:], lic, ALU.mult))
            P("dve", lambda v: v.tensor_tensor(rhoc[:], dtc[:], lrc, ALU.mult))
            P("act", lambda a: a.activation(rhoc[:], rhoc[:], AF.Exp))
            J = Jf if d == 0 else Jb
            msk = mf if d == 0 else mb
            for s in range(4):
                P("dve", lambda v, s=s: v.tensor_scalar(ang[:], J[:], thc[:, s:s + 1], None, ALU.mult))
                emit_sin(fw, sinT[d][:, s, :], BP, ang[:], BP, 0.0, kt_[:], BP)
                emit_sin(fw, cosT[d][:, s, :], BP, ang[:], BP, 0.5 * PI, kt_[:], BP)
                P("dve", lambda v, s=s: v.tensor_scalar(rmask[d][:, s, :], msk[:], rhoc[:, s:s + 1], None, ALU.mult))
            P("dve", lambda v: v.tensor_scalar(angc[:], thc[:], float(TS), None, ALU.mult))
            emit_sin(fw, G[d][:, 1, :], BP, angc[:], BP, 0.0, kc[:], BP)
            emit_sin(fw, G[d][:, 0, :], BP, angc[:], BP, 0.5 * PI, kc[:], BP)
            P("dve", lambda v: v.tensor_tensor(G[d][:, 0, :], G[d][:, 0, :], rhoc[:], ALU.mult))
            P("dve", lambda v: v.tensor_tensor(G[d][:, 1, :], G[d][:, 1, :], rhoc[:], ALU.mult))
            lrr = lr_[:, d * 1536 + 0:d * 1536 + 512]
            lir = lr_[:, d * 1536 + 512:d * 1536 + 1024]
            ldr = lr_[:, d * 1536 + 1024:d * 1536 + 1536]
            P("act", lambda a: a.activation(r1[:], ldr, AF.Exp))
            P("dve", lambda v: v.tensor_tensor(r2[:], r1[:], lrr, ALU.mult))
            P("act", lambda a: a.activation(r2[:], r2[:], AF.Exp))
            P("dve", lambda v: v.tensor_tensor(r3[:], r1[:], lir, ALU.mult))
            emit_sin(fw, r4[:], BP, r3[:], BP, 0.0, kr[:], BP)
            emit_sin(fw, r5[:], BP, r3[:], BP, 0.5 * PI, kr[:], BP)
            P("dve", lambda v: v.tensor_tensor(r4[:], r4[:], r2[:], ALU.mult))
            P("dve", lambda v: v.tensor_tensor(r5[:], r5[:], r2[:], ALU.mult))
            P("dve", lambda v: v.tensor_scalar_add(r5[:], r5[:], -1.0))
            P("dve", lambda v: v.tensor_tensor(r1[:], lrr, lrr, ALU.mult))
            P("dve", lambda v: v.tensor_tensor(r2[:], lir, lir, ALU.mult))
            P("dve", lambda v: v.tensor_tensor(r1[:], r1[:], r2[:], ALU.add))
            P("dve", lambda v: v.reciprocal(r1[:], r1[:]))
            P("dve", lambda v: v.tensor_tensor(r2[:], r5[:], lrr, ALU.mult))
            P("dve", lambda v: v.tensor_tensor(r3[:], r4[:], lir, ALU.mult))
            P("dve", lambda v: v.tensor_tensor(r2[:], r2[:], r3[:], ALU.add))
            P("dve", lambda v: v.tensor_tensor(r2[:], r2[:], r1[:], ALU.mult))
            P("dve", lambda v: v.tensor_tensor(r3[:], r4[:], lrr, ALU.mult))
            P("dve", lambda v: v.tensor_tensor(r6[:], r5[:], lir, ALU.mult))
            P("dve", lambda v: v.tensor_tensor(r3[:], r3[:], r6[:], ALU.subtract))
            P("dve", lambda v: v.tensor_tensor(r3[:], r3[:], r1[:], ALU.mult))
            P("dve", lambda v: v.tensor_tensor(r4[:], BT[:, 0:512], r2[:], ALU.mult))
            P("dve", lambda v: v.tensor_tensor(r5[:], BT[:, 512:1024], r3[:], ALU.mult))
            P("dve", lambda v: v.tensor_tensor(BbT[d][:, 0, :], r4[:], r5[:], ALU.subtract))
            P("dve", lambda v: v.tensor_tensor(r4[:], BT[:, 0:512], r3[:], ALU.mult))
            P("dve", lambda v: v.tensor_tensor(r5[:], BT[:, 512:1024], r2[:], ALU.mult))
            P("dve", lambda v: v.tensor_tensor(BbT[d][:, 1, :], r4[:], r5[:], ALU.add))

        fw.barrier()
        def _wv(i):
            return big[:, i * 2048:(i + 1) * 2048].rearrange("p (c s t) -> p c s t", c=2, s=4)
        W, BW = _wv(0), Buf("W")
        R, BR = _wv(1), Buf("R")
        Ss = [cx.sb(f"S16_{i}", [128, 2, 4, TS], BF16) for i in range(2)]
        bigb = big[:].bitcast(BF16)
        u16 = bigb[:, 8192:12288]
        CT16 = bigb[:, 12288:13312].rearrange("p (c s h) -> p c s h", c=2, s=4)
        BbT16 = [bigb[:, 13312 + d * 1024:13312 + (d + 1) * 1024].rearrange("p (c n) -> p c n", c=2) for d in range(2)]
        B16 = Buf("b16consts")
        fw.op("pool", lambda g: g.tensor_copy(u16, u[:]), reads=[Bu], writes=[B16])
        fw.op("dve", lambda v: v.tensor_copy(CT16, CTv), reads=[BP], writes=[B16])
        for d in range(2):
            fw.op("dve", lambda v, d=d: v.tensor_copy(BbT16[d], BbT[d][:]), reads=[BP], writes=[B16])
        tq = [cx.sb(f"tq{i}", [128, 2, TS]) for i in range(4)]
        uq = [cx.sb(f"uq{i}", [128, 4, TS]) for i in range(4)]
        cp, Bcp = cx.sb("cp", [128, 2, 4])
        cq, Bcq = cx.sb("cq", [128, 4, 4])
        pts = [cx.sb(f"pt{i}", [128, 512], BF16) for i in range(3)]
        o0 = [cx.sb(f"o0_{i}", [128, 128]) for i in range(2)]
        o1 = [cx.sb(f"o1_{i}", [128, 128]) for i in range(2)]
        ybs = [cx.sb(f"ybs{i}", [128, 128]) for i in range(2)]
        junk, Bjunk = cx.sb("junk", [128, 128])
        rs_ = [cx.sb(f"rs{i}", [128, 4]) for i in range(2)]

        def ssm_unit(ui):
            d = ui // NCH
            ci = ui % NCH
            c = ci if d == 0 else NCH - 1 - ci
            t0 = c * TS
            first = 0 if d == 0 else TS - 1
            last = TS - 1 if d == 0 else 0
            S, BS = Ss[ui % 2]
            if ci > 0:
                fw.op("pool", lambda g: g.tensor_tensor(cq[:, 0, :], G[d][:, 0, :], R[:, 0, :, last], ALU.mult), reads=[BP, BR], writes=[Bcq])
                fw.op("pool", lambda g: g.tensor_tensor(cq[:, 1, :], G[d][:, 1, :], R[:, 1, :, last], ALU.mult), reads=[BP, BR, Bcq], writes=[Bcq])
                fw.op("pool", lambda g: g.tensor_tensor(cq[:, 2, :], G[d][:, 0, :], R[:, 1, :, last], ALU.mult), reads=[BP, BR, Bcq], writes=[Bcq])
                fw.op("pool", lambda g: g.tensor_tensor(cq[:, 3, :], G[d][:, 1, :], R[:, 0, :, last], ALU.mult), reads=[BP, BR, Bcq], writes=[Bcq])
                fw.op("pool", lambda g: g.tensor_tensor(cp[:, 0, :], cq[:, 0, :], cq[:, 1, :], ALU.subtract), reads=[Bcq], writes=[Bcp])
                fw.op("pool", lambda g: g.tensor_tensor(cp[:, 1, :], cq[:, 2, :], cq[:, 3, :], ALU.add), reads=[Bcq, Bcp], writes=[Bcp])
            for hf in range(2):
                for si in range(2):
                    s = hf * 2 + si
                    mm(fw, psBU[:, si, 0:TS], BbT16[d][:, 0, s * 128:(s + 1) * 128], u16[:, t0:t0 + TS], True, True, [B16], [BpsBU])
                    mm(fw, psBU[:, si, TS:2 * TS], BbT16[d][:, 1, s * 128:(s + 1) * 128], u16[:, t0:t0 + TS], True, True, [B16], [BpsBU])
                bre = psBU[:, :, 0:TS]
                bim = psBU[:, :, TS:2 * TS]
                cs = cosT[d][:, hf * 2:hf * 2 + 2, :]
                sn = sinT[d][:, hf * 2:hf * 2 + 2, :]
                (t1, B1), (t2, B2), (t3, B3), (t4, B4) = tq
                fw.op("dve", lambda v: v.tensor_tensor(t1[:], bre, cs, ALU.mult), reads=[BpsBU, BP], writes=[B1])
                fw.op("dve", lambda v: v.tensor_tensor(t2[:], bim, sn, ALU.mult), reads=[BpsBU, BP], writes=[B2])
                fw.op("dve", lambda v: v.tensor_tensor(t3[:], bim, cs, ALU.mult), reads=[BpsBU, BP], writes=[B3])
                fw.op("dve", lambda v: v.tensor_tensor(t4[:], bre, sn, ALU.mult), reads=[BpsBU, BP], writes=[B4])
                fw.op("pool", lambda g: g.tensor_tensor(W[:, 0, hf * 2:hf * 2 + 2, :], t1[:], t2[:], ALU.add), reads=[B1, B2], writes=[BW])
                fw.op("pool", lambda g: g.tensor_tensor(W[:, 1, hf * 2:hf * 2 + 2, :], t3[:], t4[:], ALU.subtract), reads=[B3, B4], writes=[BW])
            if ci > 0:
                fw.op("pool", lambda g: g.tensor_tensor(W[:, 0, :, first], W[:, 0, :, first], cp[:, 0, :], ALU.add), reads=[BW, Bcp], writes=[BW])
                fw.op("pool", lambda g: g.tensor_tensor(W[:, 1, :, first], W[:, 1, :, first], cp[:, 1, :], ALU.add), reads=[BW, Bcp], writes=[BW])
            for c2 in range(2):
                o_ = R[:, c2, :, :].rearrange("p s t -> p (s t)")
                m_ = rmask[d][:].rearrange("p s t -> p (s t)")
                w_ = W[:, c2, :, :].rearrange("p s t -> p (s t)")
                if d == 1:
                    o_, m_, w_ = rev_ap(o_), rev_ap(m_), rev_ap(w_)
                fw.op("dve", lambda v, o_=o_, m_=m_, w_=w_: v.tensor_tensor_scan(o_, m_, w_, 0.0, ALU.mult, ALU.add), reads=[BP, BW], writes=[BR])
            (u1, Bu1), (u2, Bu2), (u3, Bu3), (u4, Bu4) = uq
            fw.op("pool", lambda g: g.tensor_tensor(u1[:], R[:, 0, :, :], cosT[d][:], ALU.mult), reads=[BR, BP], writes=[Bu1])
            fw.op("pool", lambda g: g.tensor_tensor(u2[:], R[:, 1, :, :], sinT[d][:], ALU.mult), reads=[BR, BP], writes=[Bu2])
            fw.op("pool", lambda g: g.tensor_tensor(S[:, 0, :, :], u1[:], u2[:], ALU.subtract), reads=[Bu1, Bu2], writes=[BS])
            fw.op("dve", lambda v: v.tensor_tensor(u3[:], R[:, 1, :, :], cosT[d][:], ALU.mult), reads=[BR, BP], writes=[Bu3])
            fw.op("dve", lambda v: v.tensor_tensor(u4[:], R[:, 0, :, :], sinT[d][:], ALU.mult), reads=[BR, BP], writes=[Bu4])
            fw.op("dve", lambda v: v.tensor_tensor(S[:, 1, :, :], u3[:], u4[:], ALU.add), reads=[Bu3, Bu4], writes=[BS])
            return (S, BS, d, t0)

        def ssm_back(st):
            S, BS, d, t0 = st
            n = 0
            for s in range(4):
                for c2 in range(2):
                    mm(fw, psY[:, 0:TS], CT16[:, c2, s, :], S[:, c2, s, :], n == 0, n == 7, [B16, BS], [BpsY])
                    n += 1
            if d == 0:
                fw.op("dve", lambda v: v.tensor_copy(yacc[:, t0:t0 + TS], psY[:, 0:TS]), reads=[BpsY], writes=[Byacc])
            else:
                fw.op("dve", lambda v: v.tensor_tensor(yacc[:, t0:t0 + TS], psY[:, 0:TS], yacc[:, t0:t0 + TS], ALU.add), reads=[BpsY, Byacc], writes=[Byacc])

        accsb = [cx.sb(f"accsb{i}", [128, 132]) for i in range(2)]

        def attn_qk(ai, kp):
            gstep = ai * 16 + kp
            hh = ai // 32
            qb = (ai // 2) % 16
            c = ai % 2
            q0 = qb * 256
            psS, BpsS = sbanks[gstep % 3]
            for e in range(2):
                kt = kp * 2 + e
                mm(fw, psS[:, e * 256:(e + 1) * 256], ks[c * 64:(c + 1) * 64, hh, kt * 128:(kt + 1) * 128],
                   qs[c * 64:(c + 1) * 64, hh, q0:q0 + 256], True, True, [Bks, Bqs], [BpsS])

        def attn_pv(ai, kp, gi):
            hh = ai // 32
            psS, BpsS = sbanks[gi % 3]
            pt, Bpt = pts[gi % 3]
            fw.op("act", lambda a: a.activation(pt[:], psS[:], AF.Exp, scale=0.125), reads=[BpsS], writes=[Bpt])
            for e in range(2):
                kt = kp * 2 + e
                for sb_ in range(2):
                    acc, Bacc = accs[sb_]
                    mm(fw, acc[:, 0:129], pt[:, e * 256 + sb_ * 128:e * 256 + (sb_ + 1) * 128], Vaug[:, hh, kt, 0:129],
                       kt == 0, kt == 31, [Bpt, BVaug], [Bacc])

        def attn_fin(ai):
            hh = ai // 32
            qb = (ai // 2) % 16
            c = ai % 2
            q0 = qb * 256
            for sb_ in range(2):
                acc, Bacc = accs[sb_]
                asb, Basb = accsb[sb_]
                fw.op("act", lambda a: a.copy(asb[:, 0:129], acc[:, 0:129]), reads=[Bacc], writes=[Basb])
            for sb_ in range(2):
                asb, Basb = accsb[sb_]
                rs, Brs = rs_[sb_]
                fw.op("dve", lambda v: v.reciprocal(rs[:, 0:1], asb[:, 128:129]), reads=[Basb], writes=[Brs])
                oo, Boo = o0[sb_]
                if c == 0:
                    fw.op("dve", lambda v: v.tensor_scalar(oo[:], asb[:, 0:128], rs[:, 0:1], None, ALU.mult), reads=[Basb, Brs], writes=[Boo])
                else:
                    o_, Bo_ = o1[sb_]
                    fw.op("dve", lambda v: v.tensor_tensor(rs[:, 1:2], rs[:, 0:1], neglam[:], ALU.mult), reads=[Brs, Bneglam], writes=[Brs])
                    fw.op("dve", lambda v: v.scalar_tensor_tensor(o_[:], asb[:, 0:128], rs[:, 1:2], oo[:], ALU.mult, ALU.add),
                          reads=[Basb, Brs, Boo], writes=[Bo_])
                    fw.op("pool", lambda g: g.tensor_tensor(junk[:], o_[:], o_[:], ALU.mult), reads=[Bo_], writes=[Bjunk])
                    fw.op("dve", lambda v: v.reduce_sum(rs[:, 2:3], junk[:], axis=AX.X), reads=[Bjunk, Brs], writes=[Brs])
                    fw.op("act", lambda a: a.activation(rs[:, 3:4], rs[:, 2:3], AF.Ln, bias=epsc[:, 0:1], scale=1.0 / 128.0), reads=[Brs, Bepsc], writes=[Brs])
                    fw.op("act", lambda a: a.activation(rs[:, 3:4], rs[:, 3:4], AF.Exp, scale=-0.5), reads=[Brs], writes=[Brs])
                    yy, Byy = ybs[sb_]
                    fw.op("dve", lambda v: v.scalar_tensor_tensor(yy[:], o_[:], rs[:, 3:4], gsc[:], ALU.mult, ALU.mult),
                          reads=[Bo_, Brs, Bgsc], writes=[Byy])
                    r0 = q0 + sb_ * 128
                    fw.dma("sp", yb[r0:r0 + 128, hh * 128:(hh + 1) * 128], yy[:], Byb, Byy)

        steps = [(ai, kp) for ai in range(64) for kp in range(16)]
        pending = None
        attn_qk(*steps[0])
        attn_qk(*steps[1])
        for gi, (ai, kp) in enumerate(steps):
            if gi + 2 < len(steps):
                attn_qk(*steps[gi + 2])
            attn_pv(ai, kp, gi)
            if kp == 15:
                attn_fin(ai)
                if ai % 2 == 1:
                    stt_ = ssm_unit(ai // 2)
                    if pending is not None:
                        ssm_back(pending)
                    pending = stt_
        ssm_back(pending)

        (ga_, Bga), (gb_, Bgb) = uq[0], uq[1]
        ga = _V2(ga_[:].rearrange("p s t -> p (s t)"))
        gb = _V2(gb_[:].rearrange("p s t -> p (s t)"))
        for blk in range(4):
            sl = slice(blk * 1024, (blk + 1) * 1024)
            fw.op("dve", lambda v: v.scalar_tensor_tensor(yacc[:, sl], u[:, sl], sm[:, 0:1], yacc[:, sl], ALU.mult, ALU.add), reads=[Bu, Bsm, Byacc], writes=[Byacc])
            fw.op("act", lambda a: a.activation(ga[:], yacc[:, sl], AF.Square), reads=[Byacc], writes=[Bga])
            fw.op("dve", lambda v: v.tensor_scalar(ga[:], ga[:], 0.044715, 1.0, ALU.mult, ALU.add), reads=[Bga], writes=[Bga])
            fw.op("pool", lambda g: g.tensor_tensor(gb[:], ga[:], yacc[:, sl], ALU.mult), reads=[Bga, Byacc], writes=[Bgb])
            fw.op("act", lambda a: a.activation(gb[:], gb[:], AF.Sigmoid, scale=2.0 * math.sqrt(2.0 / math.pi)), reads=[Bgb], writes=[Bgb])
            fw.op("pool", lambda g: g.tensor_tensor(gb[:], gb[:], yacc[:, sl], ALU.mult), reads=[Bgb, Byacc], writes=[Bgb])
            fw.dma("sp", yaT[:, sl], gb[:], ByaT, Bgb)
        fw.final_wait("sp", cx.out_bufs)
    return cx.nc


def _bf16(a):
    return np.ascontiguousarray(a).astype(ml_dtypes.bfloat16)


def make_M_inputs(inputs, l, q_b, k_b, v_b, u_b, bg_b, cg_b, xv_b):
    lam_init = 0.8 - 0.6 * math.exp(-0.3 * l)
    in_maps = []
    lre = inputs["s5_lambda_re"][l]; lim = inputs["s5_lambda_im"][l]; ldt = inputs["s5_log_dt"][l]
    for c in range(8):
        b, j = c // 4, c % 4
        gs = slice(8 * j, 8 * j + 8)
        lamc = np.zeros((128, 24), np.float32)
        lamr = np.zeros((2 * 3 * 512,), np.float32)
        for d in range(2):
            kinds = [lre[d, gs].reshape(512), lim[d, gs].reshape(512), np.repeat(ldt[d, gs], 64)]
            for ki, arr in enumerate(kinds):
                lamc[:, d * 12 + ki * 4:d * 12 + ki * 4 + 4] = arr.reshape(4, 128).T
                lamr[d * 1536 + ki * 512:d * 1536 + (ki + 1) * 512] = arr
        lamr = np.ascontiguousarray(np.broadcast_to(lamr[None, :], (128, 3072)))
        BT = np.zeros((128, 2, 512), np.float32)
        CT = np.zeros((128, 2, 4, 128), np.float32)
        for gl in range(8):
            g = 8 * j + gl
            for ri, (bsrc, csrc) in enumerate(((inputs["s5_b_re"], inputs["s5_c_re"]), (inputs["s5_b_im"], inputs["s5_c_im"]))):
                BT[gl * 16:(gl + 1) * 16, ri, gl * 64:(gl + 1) * 64] = bsrc[l, g].T
                tile, p0 = (gl * 64) // 128, (gl * 64) % 128
                CT[p0:p0 + 64, ri, tile, gl * 16:(gl + 1) * 16] = csrc[l, g].T
        smalls = np.zeros((128, 6), np.float32)
        smalls[:, 0] = inputs["s5_d"][l, 128 * j:128 * j + 128]
        smalls[:, 1:4] = inputs["conv_w"][l][:, 128 * j:128 * j + 128].T
        smalls[:, 4] = np.float32(lam_init)
        smalls[:, 5] = np.float32(1.0 - lam_init)
        dl = np.ascontiguousarray(np.broadcast_to(inputs["diff_lambda"][l].reshape(1, 256), (128, 256)))
        sub = np.ascontiguousarray(np.broadcast_to(inputs["diff_subln"][l].reshape(1, 128), (128, 128)))
        hs = slice(256 * j, 256 * j + 256)
        cs = slice(128 * j, 128 * j + 128)
        in_maps.append({
            "qT": np.ascontiguousarray(q_b[b][:, hs].T), "kT": np.ascontiguousarray(k_b[b][:, hs].T),
            "v": np.ascontiguousarray(v_b[b][:, hs]),
            "uT": np.ascontiguousarray(u_b[b][:, cs].T), "bgT": np.ascontiguousarray(bg_b[b][:, cs].T),
            "cgT": np.ascontiguousarray(cg_b[b][:, cs].T), "xvT": np.ascontiguousarray(xv_b[b][:, cs].T),
            "lamc": lamc, "lamr": lamr, "BT": BT.reshape(128, 1024), "CT": CT.reshape(128, 1024),
            "smalls": smalls, "dl": dl, "subln": sub})
    return in_maps


def run_M(inputs, l, q_b, k_b, v_b, u_b, bg_b, cg_b, xv_b):
    nc = _get("M", build_M)
    res = _run(nc, make_M_inputs(inputs, l, q_b, k_b, v_b, u_b, bg_b, cg_b, xv_b))
    ya = [np.concatenate([res[b * 4 + j]["yaT"].T for j in range(4)], axis=1) for b in range(2)]
    ybb = [np.concatenate([res[b * 4 + j]["yb"] for j in range(4)], axis=1) for b in range(2)]
    yc = [np.concatenate([res[b * 4 + j]["ycT"].T for j in range(4)], axis=1) for b in range(2)]
    return ya, ybb, yc


def build_C(last):
    cx = Ctx("C1" if last else "C0")
    x1T, Bx1T = cx.din("x1T", [D_MODEL, NTOK])
    yaT, ByaT = cx.din("yaT", [512, NTOK])
    ybT, BybT = cx.din("ybT", [1024, NTOK])
    ycT, BycT = cx.din("ycT", [512, NTOK])
    gn, Bgn_d = cx.din("gn", [128, 3 * NKT])
    wglu, Bwglu = cx.din("wglu", [512, 512])
    bsm, Bbsm_d = cx.din("bsm", [128, 4 + 48])
    wbr, Bwbr = cx.din("wbr", [D_MODEL, D_MODEL])
    wg, Bwg = cx.din("wg", [D_MODEL, 3 * D_MODEL])
    wout, Bwout = cx.din("wout", [D_MODEL, D_MODEL])
    w13, Bw13 = cx.din("w13", [D_MODEL, 2 * D_FF])
    w2, Bw2 = cx.din("w2", [D_FF, D_MODEL])
    xoT, BxoT = cx.dout("xoT", [D_MODEL, NTOK])
    with cx.st:
        cx.start()
        fw = cx.fw
        cx.make_banks(8)
        R1, _ = cx.sb("R1", [128, NKT * NTOK], F32)
        X, BX = R1[:].rearrange("p (k t) -> p k t", k=NKT), Buf("X")
        R1b = R1[:].bitcast(BF16)
        Y, BY = R1b[:, 0:16384].rearrange("p (k t) -> p k t", k=NKT), Buf("Y")
        yab, Byab = R1b[:, 16384:20480].rearrange("p (k t) -> p k t", k=4), Buf("yab")
        ya32, Bya32 = R1[:, 10240:14336].rearrange("p (k t) -> p k t", k=4), Buf("ya32")
        H, BH = cx.sb("H", [128, NKT, NTOK], BF16)
        MG, BMG = cx.sb("MG", [128, NKT, NTOK], BF16)
        ACTB, BACT = cx.sb("ACTB", [128, 11, NTOK], BF16)
        g, Bg = cx.sb("g", [128, 3 * NKT], F32)
        bs, Bbs = cx.sb("bs", [128, 52], F32)
        cx.sqbufs = [cx.sb(f"sq{i}", [128, 512], F32) for i in range(2)]
        cx.rstd = cx.sb("rstd", [128, 512], F32)
        cx.tmp512 = [cx.sb(f"tmp{i}", [128, 512], F32) for i in range(2)]
        gst = [[[cx.sb(f"gst{i}_{mi}_{hf}", [128, 512], BF16) for hf in range(2)] for mi in range(2)] for i in range(3)]
        tts = [cx.sqbufs[0], cx.sqbufs[1], cx.tmp512[0]]
        ws = WeightStream(cx)
        emit_consts(cx)
        fw.dma("sp", g[:], gn[:, :], Bg, Bgn_d)
        fw.dma("sp", bs[:], bsm[:, :], Bbs, Bbsm_d)
        load_X(cx, X, BX, x1T, Bx1T)
        emit_rmsnorm(cx, X, BX, g[:, 0:NKT], Bg, H, BH)
        fw.barrier()
        fw.dma("sp", ya32, yaT.rearrange("(k p) t -> p k t", p=128), Bya32, ByaT)
        fw.dma("pool", Y[:, 4:12, :], ybT.rearrange("(k p) t -> p k t", p=128), BY, BybT)
        fw.dma("pool", Y[:, 12:16, :], ycT.rearrange("(k p) t -> p k t", p=128), BY, BycT)
        fw.op("dve", lambda v: v.tensor_copy(yab, ya32), reads=[Bya32], writes=[Byab])
        (vg,), Bsg = ws.load([(wglu.rearrange("(kt p) c -> p kt c", p=128), Bwglu, 4, 512)])
        for m in range(4):
            for half in range(2):
                sl = slice(half * 512, (half + 1) * 512)
                pb, Bpb = cx.bank()
                for kt in range(4):
                    mm(fw, pb[:], vg[:, kt, m * 128:(m + 1) * 128], yab[:, kt, sl], kt == 0, kt == 3, [Bsg, Byab], [Bpb])
                sg, Bsgt = cx.tmp512[(m * 2 + half) % 2]
                fw.op("act", lambda a, sg=sg, pb=pb, m=m: a.activation(sg[:], pb[:], AF.Sigmoid, bias=bs[:, m:m + 1]), reads=[Bpb, Bbs], writes=[Bsgt])
                fw.op("dve", lambda v, sg=sg, m=m, sl=sl: v.tensor_tensor(Y[:, m, sl], ya32[:, m, sl], sg[:], ALU.mult), reads=[Bya32, Bsgt], writes=[BY])
        wbrv = wbr.rearrange("(kt p) c -> p kt c", p=128)
        wgv = wg.rearrange("(kt p) c -> p kt c", p=128)
        kranges = [(0, 4), (4, 12), (12, 16)]
        for mp in range(8):
            for i in range(3):
                c0 = i * D_MODEL + mp * 256
                (vgi,), Bs = ws.load([(wgv[:, :, c0:c0 + 256], Bwg, NKT, 256)])
                for mi in range(2):
                    m = mp * 2 + mi
                    for half in range(2):
                        sl = slice(half * 512, (half + 1) * 512)
                        pb, Bpb = cx.bank()
                        for kt in range(NKT):
                            mm(fw, pb[:], vgi[:, kt, mi * 128:(mi + 1) * 128], H[:, kt, sl], kt == 0, kt == NKT - 1, [Bs, BH], [Bpb])
                        gt, Bgt = gst[i][mi][half]
                        fw.op("act", lambda a, gt=gt, pb=pb, i=i, m=m: a.activation(gt[:], pb[:], AF.Sigmoid, bias=bs[:, 4 + i * 16 + m:4 + i * 16 + m + 1]),
                              reads=[Bpb, Bbs], writes=[Bgt])
            (vb,), Bs = ws.load([(wbrv[:, :, mp * 256:(mp + 1) * 256], Bwbr, NKT, 256)])
            for mi in range(2):
                m = mp * 2 + mi
                for half in range(2):
                    sl = slice(half * 512, (half + 1) * 512)
                    for i, (k0, k1) in enumerate(kranges):
                        pb, Bpb = cx.bank()
                        for kt in range(k0, k1):
                            mm(fw, pb[:], vb[:, kt, mi * 128:(mi + 1) * 128], Y[:, kt, sl], kt == k0, kt == k1 - 1, [Bs, BY], [Bpb])
                        tti, Btti = tts[i]
                        gt, Bgt = gst[i][mi][half]
                        fw.op("dve", lambda v, tti=tti, pb=pb, gt=gt: v.tensor_tensor(tti[:], pb[:], gt[:], ALU.mult), reads=[Bpb, Bgt], writes=[Btti])
                    fw.op("pool", lambda gp: gp.tensor_tensor(tts[0][0][:], tts[0][0][:], tts[1][0][:], ALU.add), reads=[tts[0][1], tts[1][1]], writes=[tts[0][1]])
                    fw.op("pool", lambda gp, m=m, sl=sl: gp.tensor_tensor(MG[:, m, sl], tts[0][0][:], tts[2][0][:], ALU.add), reads=[tts[0][1], tts[2][1]], writes=[BMG])
        fw.barrier()
        load_X(cx, X, BX, x1T, Bx1T)
        woutv = wout.rearrange("(kt p) c -> p kt c", p=128)
        for cg in range(8):
            (vw,), Bs = ws.load([(woutv[:, :, cg * 256:(cg + 1) * 256], Bwout, NKT, 256)])
            for gi in range(2):
                m = cg * 2 + gi
                for half in range(2):
                    sl = slice(half * 512, (half + 1) * 512)
                    po, Bpo = cx.bank()
                    for kt in range(NKT):
                        mm(fw, po[:], vw[:, kt, gi * 128:(gi + 1) * 128], MG[:, kt, sl], kt == 0, kt == NKT - 1, [Bs, BMG], [Bpo])
                    fw.op("dve", lambda v, po=po, m=m, sl=sl: v.tensor_tensor(X[:, m, sl], po[:], X[:, m, sl], ALU.add), reads=[Bpo, BX], writes=[BX])
        emit_rmsnorm(cx, X, BX, g[:, NKT:2 * NKT], Bg, H, BH)
        emit_ffn(cx, ws, X, BX, H, BH, w13, Bw13, w2, Bw2, ACTB, BACT)
        if last:
            emit_rmsnorm(cx, X, BX, g[:, 2 * NKT:3 * NKT], Bg, X, BX)
        store_X(cx, X, BX, xoT, BxoT)
        fw.final_wait("sp", cx.out_bufs)
    return cx.nc


def run_C(inputs, l, x1T_shards, ya, ybb, yc, last):
    nc = _get("C1" if last else "C0", lambda: build_C(last))
    nw = inputs["norm_w"][l]
    gn = np.ascontiguousarray(np.concatenate([nw[1].reshape(NKT, 128).T, nw[2].reshape(NKT, 128).T,
                                              inputs["final_norm"].reshape(NKT, 128).T], axis=1))
    bsm = np.ascontiguousarray(np.concatenate([inputs["s5_b_glu"][l].reshape(4, 128).T,
                                               inputs["b_gate"][l].reshape(48, 128).T], axis=1))
    in_maps = []
    for c in range(8):
        b, j = c // 4, c % 4
        ts_ = slice(j * NTOK, (j + 1) * NTOK)
        in_maps.append({"x1T": x1T_shards[c],
                        "yaT": np.ascontiguousarray(ya[b][ts_].T), "ybT": np.ascontiguousarray(ybb[b][ts_].T),
                        "ycT": np.ascontiguousarray(yc[b][ts_].T), "gn": gn, "bsm": bsm,
                        "wglu": np.ascontiguousarray(inputs["s5_w_glu"][l]), "wbr": np.ascontiguousarray(inputs["w_branch"][l]),
                        "wg": np.ascontiguousarray(inputs["w_gate"][l]), "wout": np.ascontiguousarray(inputs["w_out"][l]),
                        "w13": np.ascontiguousarray(inputs["ffn_w13"][l, 1]), "w2": np.ascontiguousarray(inputs["ffn_w2"][l, 1])})
    res = _run(nc, in_maps)
    return [res[c]["xoT"] for c in range(8)]


def kernel(**inputs):
    inputs = {k: np.asarray(v) for k, v in inputs.items()}
    x = inputs["x"].reshape(BATCH * SEQ, D_MODEL)
    xT = [np.ascontiguousarray(x[c * NTOK:(c + 1) * NTOK].T) for c in range(8)]
    for l in range(DEPTH):
        resA = run_A(xT, inputs, l)
        x1T = [resA[c]["x1T"] for c in range(8)]

        def cat(name, rows):
            return [np.concatenate([resA[b * 4 + j][name][rows].T for j in range(4)], axis=0) for b in range(BATCH)]
        q_b = cat("qkT", slice(0, 1024)); k_b = cat("qkT", slice(1024, 2048)); v_b = cat("vT", slice(0, 1024))
        u_b = cat("restT", slice(0, 512)); bg_b = cat("restT", slice(512, 1024))
        cg_b = cat("restT", slice(1024, 1536)); xv_b = cat("restT", slice(1536, 2048))
        ya, ybb, yc = run_M(inputs, l, q_b, k_b, v_b, u_b, bg_b, cg_b, xv_b)
        xT = run_C(inputs, l, x1T, ya, ybb, yc, last=(l == DEPTH - 1))
    out = np.concatenate([xT[c].T for c in range(8)], axis=0).reshape(BATCH, SEQ, D_MODEL)
    return np.ascontiguousarray(out.astype(np.float32))
```
